# Optimizing a Trainium2 kernel written in Bass

```python
import jax, jax.numpy as jnp
from jax import lax
import numpy as np

D_MODEL = 1024
BATCH = 16
SEQ = 4096
DEPTH = 1

D_MIX = D_MODEL
HG_WIDTH = D_MIX // 2
HG_HEADS = 4
HG_DK = HG_WIDTH // HG_HEADS
HG_DV = HG_WIDTH // HG_HEADS
HGRN_CHUNK = 16
GM_WIDTH = D_MIX - HG_WIDTH
GM_GROUPS = 4
GM_GROUP_DIM = GM_WIDTH // GM_GROUPS
GM_CHUNK = 128
PROJ_SIZES = (HG_WIDTH, HG_WIDTH, HG_WIDTH, HG_WIDTH, HG_WIDTH, GM_WIDTH, GM_WIDTH)
PROJ_TOTAL = 5 * HG_WIDTH + 2 * GM_WIDTH
PEER_HEADS = 8
PEER_N_KEYS = 128
PEER_N_EXPERTS = PEER_N_KEYS * PEER_N_KEYS
PEER_TOPK = 16
PEER_QUERY_DIM = 256
PEER_HALF = PEER_QUERY_DIM // 2
PEER_BLOCK = 128
DEEPNORM_ALPHA = (2 * DEPTH) ** 0.25
DEEPNORM_BETA = (8 * DEPTH) ** -0.25
LN_EPS = 1e-5
RMS_EPS = 1e-6

kernel_name = "hybrid_hgrn2_gmlp_peer_deepnorm"


def _layer_norm(x, g, b):
    xf = x.astype(jnp.float32)
    mu = jnp.mean(xf, axis=-1, keepdims=True)
    var = jnp.mean(jnp.square(xf - mu), axis=-1, keepdims=True)
    y = (xf - mu) * lax.rsqrt(var + LN_EPS) * g.astype(jnp.float32) + b.astype(jnp.float32)
    return y.astype(x.dtype)


def _gla_chunkwise(q, k, v, logf):
    B, H, S, DK = q.shape
    DV = v.shape[-1]
    n = S // HGRN_CHUNK
    q = q.reshape(B, H, n, HGRN_CHUNK, DK)
    k = k.reshape(B, H, n, HGRN_CHUNK, DK)
    v = v.reshape(B, H, n, HGRN_CHUNK, DV)
    logf = logf.reshape(B, H, n, HGRN_CHUNK, DK)
    b = jnp.cumsum(logf, axis=3)
    b_last = b[:, :, :, -1:, :]
    q_rel = q * jnp.exp(b - b_last)
    k_rel = k * jnp.exp(b_last - b)
    scores = jnp.einsum('bhntd,bhnsd->bhnts', q_rel, k_rel)
    tri = jnp.tril(jnp.ones((HGRN_CHUNK, HGRN_CHUNK), dtype=bool))
    o_intra = jnp.einsum('bhnts,bhnse->bhnte', jnp.where(tri, scores, 0.0), v)
    q_inter = q * jnp.exp(b)
    decay = jnp.exp(b_last[:, :, :, 0, :])

    def step(state, xs):
        qi, ki, vi, dc = xs
        o = jnp.einsum('bhtd,bhde->bhte', qi, state)
        state = state * dc[..., None] + jnp.einsum('bhsd,bhse->bhde', ki, vi)
        return state, o

    xs = (jnp.moveaxis(q_inter, 2, 0), jnp.moveaxis(k_rel, 2, 0),
          jnp.moveaxis(v, 2, 0), jnp.moveaxis(decay, 2, 0))
    state0 = jnp.zeros((B, H, DK, DV), q.dtype)
    _, o_inter = lax.scan(step, state0, xs)
    o = o_intra + jnp.moveaxis(o_inter, 0, 2)
    return o.reshape(B, H, S, DV)


def _hybrid_mixer(x, w_in, lb, hg_norm_g, gm_ln_g, gm_ln_b, gm_ws, gm_bs, w_out):
    B, S, _ = x.shape
    proj = jnp.einsum('bsd,de->bse', x, w_in)
    splits = [int(c) for c in np.cumsum(PROJ_SIZES)[:-1]]
    q, z_f, z_b, i_in, g, u, v = jnp.split(proj, splits, axis=-1)

    def heads(t):
        return t.astype(jnp.float32).reshape(B, S, HG_HEADS, -1).transpose(0, 2, 1, 3)

    qh, vh = heads(q), heads(i_in)

    def forget(z, lb_dir):
        z = heads(z)
        lb_dir = lb_dir.reshape(HG_HEADS, 1, HG_DK)
        logf = jnp.log(lb_dir + (1.0 - lb_dir) * jax.nn.sigmoid(z))
        k = (1.0 - lb_dir) * jax.nn.sigmoid(-z)
        return k, logf

    k_f, logf_f = forget(z_f, lb[0])
    k_b, logf_b = forget(z_b, lb[1])
    o_f = _gla_chunkwise(qh, k_f, vh, logf_f)
    flip = lambda t: jnp.flip(t, axis=2)
    o_b = flip(_gla_chunkwise(flip(qh), flip(k_b), flip(vh), flip(logf_b)))
    o = o_f + o_b
    o = o * lax.rsqrt(jnp.mean(jnp.square(o), axis=-1, keepdims=True) + RMS_EPS)
    o = o.transpose(0, 2, 1, 3).reshape(B, S, HG_WIDTH)
    y_hg = (o * hg_norm_g.astype(jnp.float32) * jax.nn.silu(g.astype(jnp.float32))).astype(x.dtype)

    u = jax.nn.gelu(u)
    v = jax.nn.gelu(v)
    n = S // GM_CHUNK
    vc = v.reshape(B, n, GM_CHUNK, GM_GROUPS, GM_GROUP_DIM)
    vc = _layer_norm(vc, gm_ln_g, gm_ln_b)
    sp = jnp.einsum('gpq,bnqgc->bnpgc', gm_ws, vc) + gm_bs.T[:, :, None]
    y_gm = u * sp.reshape(B, S, GM_WIDTH)

    y = jnp.concatenate([y_hg, y_gm], axis=-1)
    return jnp.einsum('bse,ed->bsd', y, w_out)


def _peer(x, wq, k1, k2, u_tab, v_tab):
    B, S, D = x.shape
    blocks = x.reshape(-1, PEER_BLOCK, D)

    def one_block(xb):
        T = xb.shape[0]
        qb = jnp.einsum('td,de->te', xb, wq).reshape(T, PEER_HEADS, 2, PEER_HALF)
        s1 = jnp.einsum('thd,hkd->thk', qb[:, :, 0], k1)
        s2 = jnp.einsum('thd,hkd->thk', qb[:, :, 1], k2)
        v1, i1 = lax.top_k(s1, PEER_TOPK)
        v2, i2 = lax.top_k(s2, PEER_TOPK)
        cand = (v1[..., :, None] + v2[..., None, :]).reshape(T, PEER_HEADS, PEER_TOPK * PEER_TOPK)
        cand_idx = (i1[..., :, None] * PEER_N_KEYS + i2[..., None, :]).reshape(T, PEER_HEADS, PEER_TOPK * PEER_TOPK)
        top_s, pos = lax.top_k(cand, PEER_TOPK)
        expert = jnp.take_along_axis(cand_idx, pos, axis=-1).reshape(T, PEER_HEADS * PEER_TOPK)
        gate = jax.nn.softmax(top_s.astype(jnp.float32), axis=-1).astype(xb.dtype)
        gate = gate.reshape(T, PEER_HEADS * PEER_TOPK)
        hid = jnp.einsum('td,tkd->tk', xb, u_tab[expert])
        act = gate * jax.nn.gelu(hid)
        return jnp.einsum('tk,tkd->td', act, v_tab[expert])

    out = lax.map(one_block, blocks)
    return out.reshape(B, S, D)


def setup_inputs(seed: int = 0) -> dict:
    key = jax.random.key(seed)
    ks = jax.random.split(key, 20)
    f32 = jnp.float32
    nrm = lambda k, shape, s: jax.random.normal(k, shape, f32) * s
    x = jax.random.normal(ks[0], (BATCH, SEQ, D_MODEL), f32)
    col_scale = jnp.concatenate([
        jnp.ones((3 * HG_WIDTH,), f32), jnp.full((HG_WIDTH,), DEEPNORM_BETA, f32),
        jnp.ones((HG_WIDTH,), f32), jnp.full((GM_WIDTH,), DEEPNORM_BETA, f32),
        jnp.ones((GM_WIDTH,), f32)])
    w_in = nrm(ks[1], (DEPTH, D_MODEL, PROJ_TOTAL), D_MODEL ** -0.5) * col_scale
    hgrn_lb_logits = nrm(ks[2], (2, DEPTH + 1, HG_WIDTH), 0.1)
    hgrn_norm_g = 1.0 + nrm(ks[3], (DEPTH, HG_WIDTH), 0.02)
    gmlp_ln_g = 1.0 + nrm(ks[4], (DEPTH, GM_GROUPS, GM_GROUP_DIM), 0.02)
    gmlp_ln_b = nrm(ks[5], (DEPTH, GM_GROUPS, GM_GROUP_DIM), 0.02)
    gmlp_ws = nrm(ks[6], (DEPTH, GM_GROUPS, GM_CHUNK, GM_CHUNK), GM_CHUNK ** -0.5)
    gmlp_bs = 1.0 + nrm(ks[7], (DEPTH, GM_GROUPS, GM_CHUNK), 0.02)
    w_out = nrm(ks[8], (DEPTH, D_MIX, D_MODEL), DEEPNORM_BETA * D_MIX ** -0.5)
    ln1_g = 1.0 + nrm(ks[9], (DEPTH, D_MODEL), 0.02)
    ln1_b = nrm(ks[10], (DEPTH, D_MODEL), 0.02)
    peer_wq = nrm(ks[11], (DEPTH, D_MODEL, PEER_HEADS * PEER_QUERY_DIM), D_MODEL ** -0.5)
    peer_k1 = nrm(ks[12], (DEPTH, PEER_HEADS, PEER_N_KEYS, PEER_HALF), PEER_HALF ** -0.5)
    peer_k2 = nrm(ks[13], (DEPTH, PEER_HEADS, PEER_N_KEYS, PEER_HALF), PEER_HALF ** -0.5)
    peer_u = nrm(ks[14], (DEPTH, PEER_N_EXPERTS, D_MODEL), D_MODEL ** -0.5)
    peer_v = nrm(ks[15], (DEPTH, PEER_N_EXPERTS, D_MODEL), DEEPNORM_BETA * (PEER_HEADS * PEER_TOPK) ** -0.5)
    ln2_g = 1.0 + nrm(ks[16], (DEPTH, D_MODEL), 0.02)
    ln2_b = nrm(ks[17], (DEPTH, D_MODEL), 0.02)
    return {"x": x, "w_in": w_in, "hgrn_lb_logits": hgrn_lb_logits, "hgrn_norm_g": hgrn_norm_g,
            "gmlp_ln_g": gmlp_ln_g, "gmlp_ln_b": gmlp_ln_b, "gmlp_ws": gmlp_ws, "gmlp_bs": gmlp_bs,
            "w_out": w_out, "ln1_g": ln1_g, "ln1_b": ln1_b, "peer_wq": peer_wq,
            "peer_k1": peer_k1, "peer_k2": peer_k2, "peer_u": peer_u, "peer_v": peer_v,
            "ln2_g": ln2_g, "ln2_b": ln2_b}


def reference(x, w_in, hgrn_lb_logits, hgrn_norm_g, gmlp_ln_g, gmlp_ln_b, gmlp_ws, gmlp_bs,
              w_out, ln1_g, ln1_b, peer_wq, peer_k1, peer_k2, peer_u, peer_v, ln2_g, ln2_b):
    lb_all = jnp.cumsum(jax.nn.softmax(hgrn_lb_logits.astype(jnp.float32), axis=1), axis=1)
    h = x
    for l in range(DEPTH):
        mix = _hybrid_mixer(h, w_in[l], lb_all[:, l], hgrn_norm_g[l], gmlp_ln_g[l], gmlp_ln_b[l],
                            gmlp_ws[l], gmlp_bs[l], w_out[l])
        h = _layer_norm(DEEPNORM_ALPHA * h + mix, ln1_g[l], ln1_b[l])
        ffn = _peer(h, peer_wq[l], peer_k1[l], peer_k2[l], peer_u[l], peer_v[l])
        h = _layer_norm(DEEPNORM_ALPHA * h + ffn, ln2_g[l], ln2_b[l])
    return h
```

```python
import numpy as np
from contextlib import ExitStack
import concourse.bass as bass
import concourse.mybir as mybir
from concourse.bass_utils import run_bass_kernel_spmd

F32 = mybir.dt.float32
BF16 = mybir.dt.bfloat16
U32 = mybir.dt.uint32
AF = mybir.ActivationFunctionType
ALU = mybir.AluOpType
AX = mybir.AxisListType

D = 1024
PT = 3584
NEXP = 16384
ALPHA = 2.0 ** 0.25
LN_EPS = 1e-5
RMS_EPS = 1e-6
TT = 256
NEG = -1e30


class Sync:
    def __init__(self, nc, es):
        self.nc, self.es = nc, es
        self.engs = {"pe": nc.tensor, "act": nc.scalar, "dve": nc.vector,
                     "pool": nc.gpsimd, "sp": nc.sync}
        self.sem, self.cnt, self.inc = {}, {}, {}
        self.known = {e: {} for e in self.engs}
        self.snap = {}
        self.lastw, self.readers = {}, {}
        self.arena = []
        self.nwaits = self.nops = 0
        for e in ("pe", "act", "dve", "pool"):
            self._mk(e, 1)

    def _mk(self, s, inc):
        self.sem[s] = self.es.enter_context(self.nc.semaphore("s_" + s))
        self.cnt[s] = 0
        self.inc[s] = inc


    def _split(self, lo, hi):
        out = []
        for rec in self.arena:
            if rec[1] <= lo or hi <= rec[0]:
                out.append(rec)
                continue
            cuts = [rec[0]] + [c for c in (lo, hi) if rec[0] < c < rec[1]] + [rec[1]]
            for a, b_ in zip(cuts[:-1], cuts[1:]):
                out.append([a, b_, rec[2], dict(rec[3])])
        self.arena = out
        return [rec for rec in self.arena if lo <= rec[0] and rec[1] <= hi]

    def op(self, eng, fn, reads=(), writes=(), dma=None):
        deps = {}

        def add(sq):
            if sq is not None and deps.get(sq[0], 0) < sq[1]:
                deps[sq[0]] = sq[1]

        for r in reads:
            if isinstance(r, tuple):
                for rec in self._split(r[1], r[2]):
                    add(rec[2])
            else:
                add(self.lastw.get(r))
        for w in writes:
            if isinstance(w, tuple):
                for rec in self._split(w[1], w[2]):
                    add(rec[2])
                    for s, q in rec[3].items():
                        add((s, q))
            else:
                add(self.lastw.get(w))
                for s, q in self.readers.get(w, {}).items():
                    add((s, q))
        if dma is not None:
            if dma not in self.sem:
                self._mk(dma, 16)
            if self.cnt[dma] > 0:
                add((dma, self.cnt[dma]))
        stream = dma if dma is not None else eng
        kn = self.known[eng]
        e = self.engs[eng]
        for s, q in deps.items():
            if s == "pe" and stream == "pe":
                continue
            if kn.get(s, 0) >= q:
                continue
            e.wait_ge(self.sem[s], q * self.inc[s])
            self.nwaits += 1
            for s2, q2 in self.snap[(s, q)].items():
                if kn.get(s2, 0) < q2:
                    kn[s2] = q2
            kn[s] = q
        ins = fn(e)
        self.nops += 1
        self.cnt[stream] += 1
        seq = self.cnt[stream]
        ins.then_inc(self.sem[stream], self.inc[stream])
        self.snap[(stream, seq)] = dict(kn)
        for r in reads:
            if isinstance(r, tuple):
                ins_ = sorted(self._split(r[1], r[2]), key=lambda x: x[0])
                pos = r[1]
                for rec in ins_:
                    if rec[0] > pos:
                        self.arena.append([pos, rec[0], None, {stream: seq}])
                    rec[3][stream] = seq
                    pos = rec[1]
                if pos < r[2]:
                    self.arena.append([pos, r[2], None, {stream: seq}])
            else:
                self.readers.setdefault(r, {})[stream] = seq
        for w in writes:
            if isinstance(w, tuple):
                self._split(w[1], w[2])
                self.arena = [rec for rec in self.arena
                              if not (w[1] <= rec[0] and rec[1] <= w[2])]
                self.arena.append([w[1], w[2], (stream, seq), {}])
            else:
                self.lastw[w] = (stream, seq)
                self.readers[w] = {}
        return ins

    def finish(self, eng="sp"):
        e = self.engs[eng]
        for s, c in self.cnt.items():
            if c > 0 and self.known[eng].get(s, 0) < c:
                e.wait_ge(self.sem[s], c * self.inc[s])


class Buf:
    def __init__(self, ap, key):
        self.ap, self.key = ap, key


def build(NB, SEQ, dbg=0):
    NTI = SEQ // TT
    nc = bass.Bass("TRN2", target_bir_lowering=False)
    di = lambda n, s, dt=F32: nc.dram_tensor(n, list(s), dt, kind="ExternalInput").ap()
    x_d = di("x", [NB, SEQ, D])
    xT_d = di("xT", [NB, D, SEQ])
    win_d = di("w_in", [D, PT])
    wout_d = di("w_out", [D, D])
    wq_d = di("wq", [D, 2048])
    k1T_d = di("k1T", [128, 8, 128])
    k2T_d = di("k2T", [128, 8, 128])
    uT_d = di("uT", [128, 128, 1024])
    v_d = di("vtab", [NEXP, D])
    lbl_d = di("lbl", [128, 2, 2, 512])
    hgn_d = di("hgn", [128, 512])
    glg_d = di("glg", [128, 512])
    glb_d = di("glb", [128, 512])
    wsT_d = di("wsT", [128, 4, 128])
    bsT_d = di("bsT", [128, 4])
    lnp_d = di("lnp", [128, 4, D])
    cst_d = di("cst", [128, 5, 128])
    out_d = nc.dram_tensor("out", [NB, SEQ, D], F32, kind="ExternalOutput").ap()
    dscr = lambda n, s, dt=BF16: nc.dram_tensor(n, list(s), dt, kind="Internal").ap()
    win_s = dscr("win_s", [D, PT])
    wout_s = dscr("wout_s", [D, D])
    wq_s = dscr("wq_s", [D, 2048])
    u_s = dscr("u_s", [128, 128, 1024])
    v_s = dscr("v_s", [NEXP, D])
    ck_s = dscr("ck_s", [NB * max(NTI, 1), 128, 512], F32)

    with ExitStack() as es:
        S = Sync(nc, es)
        cnt = [0]

        def sb(shape, dt, name=None):
            cnt[0] += 1
            n = "sb_" + (name or "t%d" % cnt[0])
            t = es.enter_context(nc.sbuf_tensor(n, list(shape), dt))
            return Buf(t[:], n)

        AW = 98496
        arena_t = es.enter_context(nc.sbuf_tensor("arena", [128, AW // 4], F32))

        def av(lo, nbytes, dt, pat=None, **kw):
            ap = arena_t[:, lo // 4:(lo + nbytes) // 4]
            if dt != F32:
                ap = ap.bitcast(dt)
            if pat:
                ap = ap.rearrange(pat, **kw)
            return Buf(ap, ("A", lo, lo + nbytes))

        psb = []
        for i in range(8):
            t = es.enter_context(nc.psum_tensor("ps%d" % i, [128, 512], F32))
            psb.append(Buf(t[:], "ps%d" % i))

        def psbf(i):
            return psb[i].ap.bitcast(BF16)

        def K(*bufs):
            out = []
            for b_ in bufs:
                k_ = b_.key if isinstance(b_, Buf) else b_
                if isinstance(k_, list):
                    out.extend(k_)
                else:
                    out.append(k_)
            return out


        def mm(out, lhsT, rhs, st, sp_, R, W):
            S.op("pe", lambda e: e.matmul(out, lhsT=lhsT, rhs=rhs, start=st, stop=sp_), K(*R), K(*W))

        def tr(out, in_, ident, R, W):
            S.op("pe", lambda e: e.transpose(out, in_, ident), K(*R), K(*W))

        def act(out, in_, func, R, W, **kw):
            S.op("act", lambda e: e.activation(out=out, in_=in_, func=func, **kw), K(*R), K(*W))

        def tt(eng, out, in0, in1, op, R, W):
            S.op(eng, lambda e: e.tensor_tensor(out=out, in0=in0, in1=in1, op=op), K(*R), K(*W))

        def ts(eng, out, in0, s1, s2, op0, op1, R, W):
            if s2 is None:
                S.op(eng, lambda e: e.tensor_scalar(out=out, in0=in0, scalar1=s1, scalar2=None, op0=op0), K(*R), K(*W))
            else:
                S.op(eng, lambda e: e.tensor_scalar(out=out, in0=in0, scalar1=s1, scalar2=s2, op0=op0, op1=op1), K(*R), K(*W))

        def stt(eng, out, in0, sc, in1, op0, op1, R, W):
            S.op(eng, lambda e: e.scalar_tensor_tensor(out=out, in0=in0, scalar=sc, in1=in1, op0=op0, op1=op1), K(*R), K(*W))

        def cp(eng, out, in_, R, W):
            if eng == "act":
                S.op("act", lambda e: e.copy(out=out, in_=in_), K(*R), K(*W))
            else:
                S.op(eng, lambda e: e.tensor_copy(out=out, in_=in_), K(*R), K(*W))

        def rsq(out, in_, scale, eps, R, W):
            act(out, in_, AF.Sqrt, R, W, scale=scale, bias=eps)
            S.op("dve", lambda e: e.reciprocal(out=out, in_=out), K(*W), K(*W))

        def dma(q, out, in_, R, W, stream):
            S.op(q, lambda e: e.dma_start(out=out, in_=in_), K(*R), K(*W), dma=stream)

        cst = sb([128, 5, 128], F32, "cst")
        dma("sp", cst.ap, cst_d, [], [cst], "d_cst")
        ident_f = cst.ap[:, 0, :]
        triU = cst.ap[:, 1, :]
        triL = cst.ap[:, 2, :]
        iota16 = cst.ap[:, 4, 0:16]
        ones_c = cst.ap[:, 4, 16:17]
        cstb = sb([128, 5, 128], BF16, "cstb")
        dma("pool", cstb.ap, cst_d, [], [cstb], "d_cstb")
        ident_b = cstb.ap[:, 0, :]
        iota_b = cstb.ap[:, 3, :]
        iota3 = sb([128, 128, 16], BF16, "iota3")
        cp("dve", iota3.ap, iota_b.unsqueeze(2).to_broadcast([128, 128, 16]), [cstb], [iota3])
        lbl = av(32768, 8192, F32, "p (a b c) -> p a b c", a=2, b=2)
        dma("sp", lbl.ap, lbl_d, [], [lbl], "d_lbl")
        lb = sb([128, 2, 512], F32, "lb")
        oml = sb([128, 2, 512], F32, "oml")
        tt("dve", lb.ap, lbl.ap[:, :, 0, :], lbl.ap[:, :, 1, :], ALU.subtract, [lbl], [lb])
        act(lb.ap, lb.ap, AF.Sigmoid, [lb], [lb])
        ts("dve", oml.ap, lb.ap, -1.0, 1.0, ALU.mult, ALU.add, [lb], [oml])
        hgn = sb([128, 512], F32, "hgn")
        glg = sb([128, 512], F32, "glg")
        glb = sb([128, 512], F32, "glb")
        bsT = sb([128, 4], F32, "bsT")
        lnp = sb([128, 4, D], F32, "lnp")
        dma("sp", hgn.ap, hgn_d, [], [hgn], "d_c1")
        dma("sp", glg.ap, glg_d, [], [glg], "d_c2")
        dma("sp", glb.ap, glb_d, [], [glb], "d_c3")
        dma("sp", bsT.ap, bsT_d, [], [bsT], "d_c4")
        dma("sp", lnp.ap, lnp_d, [], [lnp], "d_c5")
        wsT = sb([128, 4, 128], BF16, "wsT")
        k1T = sb([128, 8, 128], BF16, "k1T")
        k2T = sb([128, 8, 128], BF16, "k2T")
        dma("pool", wsT.ap, wsT_d, [], [wsT], "d_c6")
        dma("pool", k1T.ap, k1T_d, [], [k1T], "d_c7")
        dma("pool", k2T.ap, k2T_d, [], [k2T], "d_c8")

        SCR = ["scr0", "scr1", "scr2", "scr3", "scr4", "scr5", "scr6", "scr7"]
        stg = [av(i * 4096, 4096, BF16) for i in range(4)]
        nst = [0]

        def cast_copy(src, dst):
            i = nst[0] % 4
            nst[0] += 1
            F_ = src.shape[1]
            dma("pool", stg[i].ap[:, 0:F_], src, [], [stg[i]], "d_stg%d" % i)
            dma("sp", dst, stg[i].ap[:, 0:F_], [stg[i]], ["scr%d" % i], "d_sto%d" % i)

        for c in range(8):
            for (c0, c1) in ((0, 2048), (2048, PT)):
                cast_copy(win_d[c * 128:(c + 1) * 128, c0:c1], win_s[c * 128:(c + 1) * 128, c0:c1])
            cast_copy(wout_d[c * 128:(c + 1) * 128, :], wout_s[c * 128:(c + 1) * 128, :])
            cast_copy(wq_d[c * 128:(c + 1) * 128, :], wq_s[c * 128:(c + 1) * 128, :])
        stg2 = [av(o_, 4096, BF16, "p (a f) -> p a f", a=2) for o_ in (65728, 69824, 90304, 94400)]
        nst2 = [0]

        def cast_copy2(src, dst):
            i = nst2[0] % 4
            nst2[0] += 1
            dma("pool", stg2[i].ap, src, [], [stg2[i]], "d_stg2%d" % i)
            dma("sp", dst, stg2[i].ap, [stg2[i]], ["scr%d" % (4 + i)], "d_sto2%d" % i)

        tab_next = [0]

        def copy_tables(n):
            n = 2 * ((n + 1) // 2)
            for a in range(tab_next[0], min(128, tab_next[0] + n), 2):
                cast_copy2(uT_d[a:a + 2].rearrange("a p f -> p a f"), u_s[a:a + 2].rearrange("a p f -> p a f"))
                cast_copy2(v_d[a * 128:(a + 2) * 128, :].rearrange("(a b) f -> b a f", b=128),
                           v_s[a * 128:(a + 2) * 128, :].rearrange("(a b) f -> b a f", b=128))
            tab_next[0] = min(128, tab_next[0] + n)


        S_f = sb([128, 512], F32, "S_f")
        S_b = sb([128, 512], F32, "S_b")
        Sbf = {k: sb([128, 512], BF16, "Sbf_%s" % k) for k in ("f0", "f1", "b0", "b1")}
        hxs = [sb([128, 2, D], F32, "hx%d" % i) for i in range(2)]
        xT = [sb([128, 8, TT], BF16, "xT%d" % i) for i in range(2)]
        h1b = [av(69824 + i * 2048, 2048, BF16) for i in range(2)]
        hxk2 = [["hx%d_c0" % i, "hx%d_c1" % i] for i in range(2)]
        h1Tk2 = [["h1T%d_c0" % i, "h1T%d_c1" % i] for i in range(2)]
        lnst = [dict(st=sb([128, 4, 6], F32, "lst%d" % i), mv=sb([128, 4, 2], F32, "lmv%d" % i), rs=sb([128, 4], F32, "lrs%d" % i), nm=sb([128, 4], F32, "lnm%d" % i)) for i in range(2)]
        h1Ts = [sb([128, 8, TT], BF16, "h1T%d" % i) for i in range(2)]
        yT = [av(65728 + i * 2048, 2048, BF16, "p (c t) -> p c t", c=8) for i in range(2)]
        eaT = sb([128, TT], BF16, "eaT")
        ebT = sb([128, TT], BF16, "ebT")
        gT = sb([128, TT], BF16, "gT")
        NUB = 5
        ubuf = [sb([128, 8, 128], BF16, "ub%d" % i) for i in range(NUB)]
        vbuf = [sb([128, D], BF16, "vb%d" % i) for i in range(NUB)]
        st8 = sb([128, 2, 6], F32, "st8")
        mv = sb([128, 2], F32, "mv")
        rstd = sb([128, 1], F32, "rstd")
        nmr = sb([128, 1], F32, "nmr")

        wsl = [av(73920 + i * 8192, 8192, BF16, "p (c e) -> p c e", c=8) for i in range(3)]
        nws = [0]

        def load_w(src_s, c0, slot=None):
            if slot is None:
                i = nws[0] % 3
                nws[0] += 1
            else:
                i = slot
            dma("sp", wsl[i].ap, src_s[:, c0:c0 + 512].rearrange("(c p) e -> p c e", p=128),
                SCR, [wsl[i]], "d_w%d" % i)
            return wsl[i]

        MB = 0
        off = [MB]

        def mt(nbytes, dt, pat=None, **kw):
            b = av(off[0], nbytes, dt, pat, **kw)
            off[0] += nbytes
            return b

        trn = {}
        for c in range(2):
            for d_ in range(2):
                trn[(c, d_)] = dict(sf=mt(2048, F32), logf=mt(2048, F32), E=mt(2048, F32), qt=mt(1024, BF16))
        pcd = {}
        for c in range(2):
            for d_ in range(2):
                pcd[(c, d_)] = dict(qtT=mt(1024, BF16, "p (h t) -> p h t", h=4),
                                    ktT=mt(1024, BF16, "p (h t) -> p h t", h=4),
                                    ktk=mt(1024, BF16), scm=mt(1024, BF16, "p (h t) -> p h t", h=4),
                                    dec=mt(32, F32))
        pc = [dict(q=mt(2048, F32), vtok=mt(1024, BF16), sg=mt(2048, F32), gu=mt(2048, F32),
                   vc=mt(1024, BF16), y=mt(2048, BF16), ss=mt(32, F32),
                   gv=trn[(c, 0)]["sf"], osb=trn[(c, 0)]["logf"], sq=trn[(c, 0)]["E"]) for c in range(2)]
        assert off[0] <= 65728, off[0]
        qT = av(0, 8192, BF16, "p (c t) -> p c t", c=16)
        oh = av(24576, 8192, F32, "p (h k i) -> p h k i", h=8, k=16)
        TK = []
        for c_, (b0, sm) in enumerate(((8192, 32768), (40960, 57344))):
            TK.append(dict(
                s12=av(b0, 8192, F32, "p (a h k) -> p a h k", a=2, h=8),
                cand=av(b0 + 8192, 8192, F32, "p (h k) -> p h k", h=8),
                v12=av(sm, 1024, F32, "p (a h k) -> p a h k", a=2, h=8),
                i12u=av(sm + 1024, 1024, U32, "p (a h k) -> p a h k", a=2, h=8),
                i12=av(sm + 2048, 1024, F32, "p (a h k) -> p a h k", a=2, h=8),
                tsv=av(sm + 3072, 512, F32, "p (h k) -> p h k", h=8),
                posu=av(sm + 3584, 512, U32, "p (h k) -> p h k", h=8),
                posf=av(sm + 4096, 512, F32, "p (h k) -> p h k", h=8),
                pjf=av(sm + 4608, 512, F32, "p (h k) -> p h k", h=8),
                pif=av(sm + 5120, 512, F32, "p (h k) -> p h k", h=8),
                eaf=av(sm + 5632, 512, F32),
                ebf=av(sm + 6144, 512, F32),
                gtf=av(sm + 6656, 512, F32, "p (h k) -> p h k", h=8),
                zs=av(sm + 7168, 32, F32)))
        TK[0]["cand"] = av(16384, 8192, F32, "p (h k) -> p h k", h=8)
        TK[1]["s12"] = av(40960, 8192, F32, "p (a h k) -> p a h k", a=2, h=8)
        TK[1]["cand"] = av(49152, 8192, F32, "p (h k) -> p h k", h=8)
        Gb = av(0, 65536, BF16, "p (t a) -> p t a", a=128)
        OAb = [av(81920 + i * 4096, 4096, BF16, "p (a t) -> p a t", t=16) for i in range(2)]
        OBb = [av(90112 + i * 4096, 4096, BF16, "p (a t) -> p a t", t=16) for i in range(2)]
        gel = [sb([128, TT], F32, "gel%d" % i) for i in range(4)]
        actb = [sb([128, TT], BF16, "actb%d" % i) for i in range(4)]

        def ln_rows(z, zkeys, gi, c):
            zz = z.ap[:, c, :]
            zk = zkeys[c]
            st = lnst[c]
            for hf in range(2):
                S.op("dve", lambda e, hf=hf: e.bn_stats(out=st["st"].ap[:, hf, :], in_=zz[:, hf * 512:(hf + 1) * 512]),
                     K(zk), K(st["st"]))
            yield
            S.op("dve", lambda e: e.bn_aggr(out=st["mv"].ap[:, 0, :], in_=st["st"].ap[:, 0:2, :].rearrange("p a b -> p (a b)")), K(st["st"]), K(st["mv"]))
            yield
            act(st["rs"].ap[:, 0:1], st["mv"].ap[:, 0, 1:2], AF.Sqrt, [st["mv"]], [st["rs"]], scale=1.0, bias=LN_EPS)
            yield
            S.op("dve", lambda e: e.reciprocal(out=st["rs"].ap[:, 0:1], in_=st["rs"].ap[:, 0:1]), K(st["rs"]), K(st["rs"]))
            yield
            ts("dve", st["nm"].ap[:, 0:1], st["mv"].ap[:, 0, 0:1], -1.0, st["rs"].ap[:, 0:1], ALU.mult, ALU.mult, [st["mv"], st["rs"]], [st["nm"]])
            yield
            act(zz, zz, AF.Identity, [zk, st["nm"], st["rs"]], [zk], scale=st["rs"].ap[:, 0:1], bias=st["nm"].ap[:, 0:1])
            yield
            tt("dve", zz, zz, lnp.ap[:, gi, :], ALU.mult, [zk, lnp], [zk])
            yield
            tt("dve", zz, zz, lnp.ap[:, gi + 1, :], ALU.add, [zk, lnp], [zk])
            yield

        def proj(xt, c, wslot, bank):
            for dc in range(8):
                mm(psb[bank].ap, xt.ap[:, dc, c * 128:(c + 1) * 128], wslot.ap[:, dc, :], dc == 0, dc == 7,
                   [xt, wslot], [psb[bank]])

        def run_il(*gens):
            gens = list(gens)
            while gens:
                for g_ in list(gens):
                    try:
                        next(g_)
                    except StopIteration:
                        gens.remove(g_)

        def gate_chain(xt, wslot, c, d_, zc, tb, toff, need_q, qsrc, bkey=None):
            bkey = bkey or (c, d_)
            T_ = trn[bkey]
            P_ = pcd[bkey]
            Z = psb[zc]
            proj(xt, c, wslot, zc)
            yield
            act(T_["sf"].ap, Z.ap, AF.Sigmoid, [Z], [T_["sf"]])
            yield
            tt("dve", T_["sf"].ap, T_["sf"].ap, oml.ap[:, d_, :], ALU.mult, [T_["sf"], oml], [T_["sf"]])
            yield
            tt("dve", T_["sf"].ap, T_["sf"].ap, lb.ap[:, d_, :], ALU.add, [T_["sf"], lb], [T_["sf"]])
            yield
            act(T_["logf"].ap, T_["sf"].ap, AF.Ln, [T_["sf"]], [T_["logf"]])
            yield
            ts("pool", T_["sf"].ap, T_["sf"].ap, -1.0, 1.0, ALU.mult, ALU.add, [T_["sf"]], [T_["sf"]])
            tri = triU if d_ == 0 else triL
            mm(Z.ap, tri, T_["logf"].ap, True, True, [cst, T_["logf"]], [Z])
            q4 = (2 * bkey[0] + bkey[1]) * 4
            for h in range(4):
                mm(psb[6].ap[:, q4 + h:q4 + h + 1], T_["logf"].ap[:, h * 128:(h + 1) * 128], ones_c, True, True,
                   [T_["logf"], cst], [psb[6]])
            yield
            act(T_["E"].ap, Z.ap, AF.Exp, [Z], [T_["E"]], scale=-1.0)
            act(P_["dec"].ap[:, 0:4], psb[6].ap[:, q4:q4 + 4], AF.Exp, [psb[6]], [P_["dec"]])
            yield
            tt("dve", P_["ktk"].ap, T_["sf"].ap, T_["E"].ap, ALU.mult, [T_["sf"], T_["E"]], [P_["ktk"]])
            yield
            if need_q:
                act(T_["E"].ap, Z.ap, AF.Exp, [Z, P_["ktk"]], [T_["E"]])
                pb = psbf(tb)
                for h in range(4):
                    tr(pb[:, toff + h * 128:toff + (h + 1) * 128], P_["ktk"].ap[:, h * 128:(h + 1) * 128], ident_b,
                       [P_["ktk"], cstb], [psb[tb]])
                yield
                cp("act", P_["ktT"].ap, pb[:, toff:toff + 512].rearrange("p (h t) -> p h t", h=4), [psb[tb]], [P_["ktT"]])
                tt("dve", T_["qt"].ap, qsrc.ap, T_["E"].ap, ALU.mult, [qsrc, T_["E"]], [T_["qt"]])
                yield
                for h in range(4):
                    tr(pb[:, toff + h * 128:toff + (h + 1) * 128], T_["qt"].ap[:, h * 128:(h + 1) * 128], ident_b,
                       [T_["qt"], cstb], [psb[tb]])
                yield
                cp("act", P_["qtT"].ap, pb[:, toff:toff + 512].rearrange("p (h t) -> p h t", h=4), [psb[tb]], [P_["qtT"]])
                yield

        def state_update(Sst, c, d_, vtok, bank, snap_to=None, bkey=None):
            P_ = pcd[bkey or (c, d_)]
            if snap_to is not None:
                cp("act", snap_to.ap, Sst.ap, [Sst], [snap_to])
            for h in range(4):
                mm(psb[bank].ap[:, h * 128:(h + 1) * 128], P_["ktk"].ap[:, h * 128:(h + 1) * 128],
                   vtok.ap[:, h * 128:(h + 1) * 128], True, True, [P_["ktk"], vtok], [psb[bank]])
            yield
            tt("dve", Sst.ap, Sst.ap, psb[bank].ap, ALU.add, [Sst, psb[bank]], [Sst])
            yield
            for h in range(4):
                act(Sst.ap[:, h * 128:(h + 1) * 128], Sst.ap[:, h * 128:(h + 1) * 128], AF.Copy, [Sst, P_["dec"]], [Sst],
                    scale=P_["dec"].ap[:, h:h + 1])
            yield

        nx = [0]

        def load_xT(b, j):
            i = nx[0] % 2
            nx[0] += 1
            dma("pool", xT[i].ap, xT_d[b, :, j * TT:(j + 1) * TT].rearrange("(c p) t -> p c t", p=128),
                [], [xT[i]], "d_xT%d" % i)
            return xT[i]

        def pass1(b):
            S.op("dve", lambda e: e.memset(S_b.ap, 0.0), [], K(S_b))
            if NTI < 2:
                dma("sp", ck_s[b * NTI], S_b.ap, [S_b], ["ck%d_0" % b], "d_cko")
                copy_tables(128)
                return
            wzb = load_w(win_s, 1024, slot=0)
            wi = load_w(win_s, 1536, slot=1)
            nper = -(-128 // max(NB * (NTI - 1), 1))

            def chains(j, st):
                xt = load_xT(b, j)
                vt = [pc[c]["vtok"] if st == 1 else pc[c]["vc"] for c in range(2)]

                def vchain(c):
                    proj(xt, c, wi, 4 + c)
                    yield
                    cp("act", vt[c].ap, psb[4 + c].ap, [psb[4 + c]], [vt[c]])
                    yield

                zb0 = 0 if st == 1 else 2
                return [gate_chain(xt, wzb, 1, 1, zb0, 4, 0, False, None, bkey=(1, st)),
                        gate_chain(xt, wzb, 0, 1, zb0 + 1, 5, 0, False, None, bkey=(0, st)),
                        vchain(1), vchain(0)]

            def updates(j, st):
                vt = [pc[c]["vtok"] if st == 1 else pc[c]["vc"] for c in range(2)]
                dma("sp", ck_s[b * NTI + j], S_b.ap, [S_b], ["ck%d_%d" % (b, j)], "d_cko")
                for c in (1, 0):
                    yield from state_update(S_b, c, 1, vt[c], 7, bkey=(c, st))

            st = 1
            run_il(*chains(NTI - 1, st))
            for j in range(NTI - 1, 0, -1):
                nxt = chains(j - 1, 1 - st) if j - 1 >= 1 else []
                run_il(updates(j, st), *nxt)
                st = 1 - st
                copy_tables(nper)
            dma("sp", ck_s[b * NTI], S_b.ap, [S_b], ["ck%d_0" % b], "d_cko")
            if b == NB - 1:
                copy_tables(128)

        pre = {}
        prewq = {}

        def mixer_prefetch(b, j):
            xt = load_xT(b, j)
            dma("sp", S_b.ap, ck_s[b * NTI + j], ["ck%d_%d" % (b, j)], [S_b], "d_cki")
            pre[(b, j)] = xt

        def mixer_tile(b, j):
            t0 = j * TT
            hx, hxk, h1T, h1Tk = hxs[j % 2], hxk2[j % 2], h1Ts[j % 2], h1Tk2[j % 2]
            if (b, j) not in pre:
                mixer_prefetch(b, j)
            xt = pre.pop((b, j))
            wq_, wi_, wg_ = load_w(win_s, 0), load_w(win_s, 3 * 512), load_w(win_s, 4 * 512)

            def evac(c, w, bank, kind):
                proj(xt, c, w, bank)
                pb = psb[bank]
                yield
                if kind == "q":
                    cp("act", pc[c]["q"].ap, pb.ap, [pb], [pc[c]["q"]])
                elif kind == "i":
                    cp("act", pc[c]["vtok"].ap, pb.ap, [pb], [pc[c]["vtok"]])
                elif kind == "g":
                    act(pc[c]["sg"].ap, pb.ap, AF.Silu, [pb], [pc[c]["sg"]])
                    yield
                    tt("pool", pc[c]["sg"].ap, pc[c]["sg"].ap, hgn.ap, ALU.mult, [pc[c]["sg"], hgn], [pc[c]["sg"]])
                elif kind == "u":
                    act(pc[c]["gu"].ap, pb.ap, AF.Gelu_apprx_tanh, [pb], [pc[c]["gu"]])
                else:
                    gv = pc[c]["gv"]
                    act(gv.ap, pb.ap, AF.Gelu_apprx_tanh, [pb], [gv])
                    yield
                    st = lnst[c]
                    for gg in range(4):
                        sl = gv.ap[:, gg * 128:(gg + 1) * 128]
                        S.op("dve", lambda e, sl=sl, gg=gg: e.bn_stats(out=st["st"].ap[:, gg, :], in_=sl), K(gv), K(st["st"]))
                    yield
                    for gg in range(4):
                        S.op("dve", lambda e, gg=gg: e.bn_aggr(out=st["mv"].ap[:, gg, :], in_=st["st"].ap[:, gg, :]), K(st["st"]), K(st["mv"]))
                    yield
                    act(st["rs"].ap, st["mv"].ap[:, :, 1], AF.Sqrt, [st["mv"]], [st["rs"]], scale=1.0, bias=LN_EPS)
                    yield
                    S.op("dve", lambda e: e.reciprocal(out=st["rs"].ap, in_=st["rs"].ap), K(st["rs"]), K(st["rs"]))
                    yield
                    stt("dve", st["nm"].ap, st["mv"].ap[:, :, 0], -1.0, st["rs"].ap, ALU.mult, ALU.mult, [st["mv"], st["rs"]], [st["nm"]])
                    yield
                    for gg in range(4):
                        sl = gv.ap[:, gg * 128:(gg + 1) * 128]
                        act(sl, sl, AF.Identity, [gv, st["nm"], st["rs"]], [gv], scale=st["rs"].ap[:, gg:gg + 1], bias=st["nm"].ap[:, gg:gg + 1])
                    yield
                    tt("pool", gv.ap, gv.ap, glg.ap, ALU.mult, [gv, glg], [gv])
                    yield
                    tt("pool", pc[c]["vc"].ap, gv.ap, glb.ap, ALU.add, [gv, glb], [pc[c]["vc"]])
                yield

            run_il(evac(0, wq_, 0, "q"), evac(1, wq_, 1, "q"), evac(0, wi_, 2, "i"), evac(1, wi_, 3, "i"),
                   evac(0, wg_, 4, "g"), evac(1, wg_, 5, "g"))
            wzf = load_w(win_s, 512)
            wzb = load_w(win_s, 1024)
            dma("sp", hx.ap, x_d[b, t0:t0 + TT, :].rearrange("(j p) d -> p j d", p=128), [], hxk, "d_hx")
            run_il(gate_chain(xt, wzf, 0, 0, 0, 4, 0, True, pc[0]["q"]), gate_chain(xt, wzf, 1, 0, 1, 5, 0, True, pc[1]["q"]),
                   gate_chain(xt, wzb, 0, 1, 2, 4, 512, True, pc[0]["q"]), gate_chain(xt, wzb, 1, 1, 3, 5, 512, True, pc[1]["q"]))
            wu_ = load_w(win_s, 5 * 512)
            wv_ = load_w(win_s, 6 * 512)

            def fwd_chain():
                yield from state_update(S_f, 0, 0, pc[0]["vtok"], 4, Sbf["f0"])
                yield from state_update(S_f, 1, 0, pc[1]["vtok"], 4, Sbf["f1"])

            def bwd_chain():
                yield from state_update(S_b, 1, 1, pc[1]["vtok"], 5, Sbf["b1"])
                cp("act", Sbf["b0"].ap, S_b.ap, [S_b], [Sbf["b0"]])
                yield

            run_il(evac(0, wu_, 0, "u"), evac(1, wu_, 1, "u"), evac(0, wv_, 2, "v"), evac(1, wv_, 3, "v"),
                   fwd_chain(), bwd_chain())
            wo = [load_w(wout_s, 0), load_w(wout_s, 512)]
            prewq[(b, j)] = load_w(wq_s, 0)

            def part2(c):
                P = pc[c]
                B0 = 4 * c
                for d_ in range(2):
                    Q = pcd[(c, d_)]
                    bk = psb[B0 + d_]
                    for h in range(4):
                        mm(bk.ap[:, h * 128:(h + 1) * 128], Q["ktT"].ap[:, h, :], Q["qtT"].ap[:, h, :], True, True,
                           [Q["ktT"], Q["qtT"]], [bk])
                    yield
                    tri = triU if d_ == 0 else triL
                    tt("dve", Q["scm"].ap, bk.ap.rearrange("p (h t) -> p h t", h=4),
                       tri.unsqueeze(1).to_broadcast([128, 4, 128]), ALU.mult, [bk, cst], [Q["scm"]])
                    yield
                ob = psb[B0 + 2]
                for h in range(4):
                    o_ = ob.ap[:, h * 128:(h + 1) * 128]
                    vs = P["vtok"].ap[:, h * 128:(h + 1) * 128]
                    Qf, Qb = pcd[(c, 0)], pcd[(c, 1)]
                    sf_, sb_ = Sbf["f%d" % c], Sbf["b%d" % c]
                    mm(o_, Qf["scm"].ap[:, h, :], vs, True, False, [Qf["scm"], P["vtok"]], [ob])
                    mm(o_, Qf["qtT"].ap[:, h, :], sf_.ap[:, h * 128:(h + 1) * 128], False, False, [Qf["qtT"], sf_], [ob])
                    mm(o_, Qb["scm"].ap[:, h, :], vs, False, False, [Qb["scm"], P["vtok"]], [ob])
                    mm(o_, Qb["qtT"].ap[:, h, :], sb_.ap[:, h * 128:(h + 1) * 128], False, True, [Qb["qtT"], sb_], [ob])
                gb_ = psb[B0 + 3]
                for gg in range(4):
                    mm(gb_.ap[:, gg * 128:(gg + 1) * 128], wsT.ap[:, gg, :], P["vc"].ap[:, gg * 128:(gg + 1) * 128],
                       True, True, [wsT, P["vc"]], [gb_])
                yield
                cp("act", P["osb"].ap, ob.ap, [ob], [P["osb"]])
                for gg in range(4):
                    stt("dve", P["y"].ap[:, 512 + gg * 128:512 + (gg + 1) * 128], gb_.ap[:, gg * 128:(gg + 1) * 128],
                        bsT.ap[:, gg:gg + 1], P["gu"].ap[:, gg * 128:(gg + 1) * 128], ALU.add, ALU.mult,
                        [gb_, bsT, P["gu"]], [P["y"]])
                yield
                sq = P["sq"]
                tt("pool", sq.ap, P["osb"].ap, P["osb"].ap, ALU.mult, [P["osb"]], [sq])
                yield
                S.op("dve", lambda e: e.tensor_reduce(out=P["ss"].ap[:, 0:4], in_=sq.ap.rearrange("p (h e) -> p h e", h=4),
                                                      axis=AX.X, op=ALU.add), K(sq), K(P["ss"]))
                yield
                act(P["ss"].ap[:, 0:4], P["ss"].ap[:, 0:4], AF.Sqrt, [P["ss"]], [P["ss"]], scale=1.0 / 128, bias=RMS_EPS)
                tt("pool", P["osb"].ap, P["osb"].ap, P["sg"].ap, ALU.mult, [P["osb"], P["sg"]], [P["osb"]])
                yield
                S.op("dve", lambda e: e.reciprocal(out=P["ss"].ap[:, 0:4], in_=P["ss"].ap[:, 0:4]), K(P["ss"]), K(P["ss"]))
                yield
                for h in range(4):
                    ts("dve", P["y"].ap[:, h * 128:(h + 1) * 128], P["osb"].ap[:, h * 128:(h + 1) * 128],
                       P["ss"].ap[:, h:h + 1], None, ALU.mult, None, [P["osb"], P["ss"]], [P["y"]])
                yield
                tb_ = psb[B0]
                pb = psbf(B0)
                for ec in range(8):
                    tr(pb[:, ec * 128:(ec + 1) * 128], P["y"].ap[:, ec * 128:(ec + 1) * 128], ident_b, [P["y"], cstb], [tb_])
                yield
                cp("act", yT[c].ap, pb[:, 0:1024].rearrange("p (c t) -> p c t", c=8), [tb_], [yT[c]])
                yield
                for hf in range(2):
                    bk = psb[B0 + 1 + hf]
                    for ec in range(8):
                        mm(bk.ap, yT[c].ap[:, ec, :], wo[hf].ap[:, ec, :], ec == 0, ec == 7, [yT[c], wo[hf]], [bk])
                yield
                for hf in range(2):
                    bk = psb[B0 + 1 + hf]
                    stt("dve", hx.ap[:, c, hf * 512:(hf + 1) * 512], hx.ap[:, c, hf * 512:(hf + 1) * 512], ALPHA,
                        bk.ap, ALU.mult, ALU.add, [hxk[c], bk], [hxk[c]])
                yield
                yield from ln_rows(hx, hxk, 0, c)
                cp("act", h1b[c].ap, hx.ap[:, c, :], [hxk[c]], [h1b[c]])
                yield
                pb = psbf(B0 + 3)
                for dc in range(8):
                    tr(pb[:, dc * 128:(dc + 1) * 128], h1b[c].ap[:, dc * 128:(dc + 1) * 128], ident_b, [h1b[c], cstb], [psb[B0 + 3]])
                yield
                cp("act", h1T.ap[:, :, c * 128:(c + 1) * 128], pb[:, 0:1024].rearrange("p (c t) -> p c t", c=8),
                   [psb[B0 + 3]], [h1Tk[c]])
                yield

            run_il(part2(0), part2(1))

        qT = av(0, 8192, BF16, "p (c t) -> p c t", c=16)
        s12c = [av(65536 + c_ * 8192, 8192, F32, "p (a h k) -> p a h k", a=2, h=8) for c_ in range(2)]
        cand = av(81920, 8192, F32, "p (h k) -> p h k", h=8)
        oh = av(81920, 8192, F32, "p (h k i) -> p h k i", h=8, k=16)
        sm = 90112
        v12 = av(sm, 1024, F32, "p (a h k) -> p a h k", a=2, h=8)
        i12u = av(sm + 1024, 1024, U32, "p (a h k) -> p a h k", a=2, h=8)
        i12 = av(sm + 2048, 1024, F32, "p (a h k) -> p a h k", a=2, h=8)
        tsv = av(sm + 3072, 512, F32, "p (h k) -> p h k", h=8)
        posu = av(sm + 3584, 512, U32, "p (h k) -> p h k", h=8)
        posf = av(sm + 4096, 512, F32, "p (h k) -> p h k", h=8)
        pjf = av(sm + 4608, 512, F32, "p (h k) -> p h k", h=8)
        pif = av(sm + 5120, 512, F32, "p (h k) -> p h k", h=8)
        eaf = av(sm + 5632, 512, F32)
        ebf = av(sm + 6144, 512, F32)
        gtf = av(sm + 6656, 512, F32, "p (h k) -> p h k", h=8)
        zs = av(sm + 7168, 32, F32)
        sk = lambda buf, idx, n, part=0, np_=1: ("A", buf.key[1] + idx * n + part * (n // np_), buf.key[1] + idx * n + (part + 1) * (n // np_))
        HH = [(hf, h) for hf in range(2) for h in range(8)]
        eaTk = ["eaT0", "eaT1"]
        ebTk = ["ebT0", "ebT1"]
        gTk = ["gT0", "gT1"]

        def peer_front(b, j):
            h1T, h1Tk = h1Ts[j % 2], h1Tk2[j % 2]
            for g in range(4):
                w = prewq.pop((b, j), None) if g == 0 else None
                if w is None:
                    w = load_w(wq_s, g * 512)
                for cc in range(4):
                    ce = g * 4 + cc
                    bank = ce % 2
                    for dc in range(8):
                        mm(psb[bank].ap[:, 0:TT], w.ap[:, dc, cc * 128:(cc + 1) * 128], h1T.ap[:, dc, :], dc == 0, dc == 7,
                           [w] + h1Tk, [psb[bank]])
                    cp("act", qT.ap[:, ce, :], psb[bank].ap[:, 0:TT], [psb[bank]], [qT])
            for c in range(2):
                PB = 4 * c
                for hf in range(2):
                    kT = k1T if hf == 0 else k2T
                    for h in range(8):
                        bank = PB + hf * 2 + h // 4
                        mm(psb[bank].ap[:, (h % 4) * 128:(h % 4 + 1) * 128], qT.ap[:, 2 * h + hf, c * 128:(c + 1) * 128],
                           kT.ap[:, h, :], True, True, [qT, kT], [psb[bank]])
                for hf in range(2):
                    for q4 in range(2):
                        bank = PB + hf * 2 + q4
                        cp("act", s12c[c].ap[:, hf, q4 * 4:(q4 + 1) * 4, :], psb[bank].ap.rearrange("p (h k) -> p h k", h=4),
                           [psb[bank]], [sk(s12c[c], hf * 2 + q4, 2048)])

        def topk_rest(c):
            s12 = s12c[c]
            for (hf, h) in HH:
                q = hf * 8 + h
                S.op("dve", lambda e, hf=hf, h=h: e.max(out=v12.ap[:, hf, h, 0:8], in_=s12.ap[:, hf, h, :]),
                     [sk(s12, q, 512)], [sk(v12, q, 64, 0, 2)])
                if q % 4 == 3:
                    yield
            for (hf, h) in HH:
                q = hf * 8 + h
                S.op("dve", lambda e, hf=hf, h=h: e.max_index(out=i12u.ap[:, hf, h, 0:8], in_max=v12.ap[:, hf, h, 0:8], in_values=s12.ap[:, hf, h, :]),
                     [sk(s12, q, 512), sk(v12, q, 64, 0, 2)], [sk(i12u, q, 64, 0, 2)])
                if q % 4 == 3:
                    yield
            for (hf, h) in HH:
                q = hf * 8 + h
                S.op("dve", lambda e, hf=hf, h=h: e.match_replace(out=s12.ap[:, hf, h, :], in_to_replace=v12.ap[:, hf, h, 0:8], in_values=s12.ap[:, hf, h, :], imm_value=NEG),
                     [sk(s12, q, 512), sk(v12, q, 64, 0, 2)], [sk(s12, q, 512)])
                if q % 4 == 3:
                    yield
            for (hf, h) in HH:
                q = hf * 8 + h
                S.op("dve", lambda e, hf=hf, h=h: e.max(out=v12.ap[:, hf, h, 8:16], in_=s12.ap[:, hf, h, :]),
                     [sk(s12, q, 512)], [sk(v12, q, 64, 1, 2)])
                if q % 4 == 3:
                    yield
            for (hf, h) in HH:
                q = hf * 8 + h
                S.op("dve", lambda e, hf=hf, h=h: e.max_index(out=i12u.ap[:, hf, h, 8:16], in_max=v12.ap[:, hf, h, 8:16], in_values=s12.ap[:, hf, h, :]),
                     [sk(s12, q, 512), sk(v12, q, 64, 1, 2)], [sk(i12u, q, 64, 1, 2)])
                if q % 4 == 3:
                    yield
            cp("dve", i12.ap, i12u.ap, [i12u], [i12])
            yield
            for h0 in (0, 4):
                tt("dve", cand.ap[:, h0:h0 + 4].rearrange("p h (i j) -> p h i j", i=16),
                   v12.ap[:, 0, h0:h0 + 4].unsqueeze(3).to_broadcast([128, 4, 16, 16]),
                   v12.ap[:, 1, h0:h0 + 4].unsqueeze(2).to_broadcast([128, 4, 16, 16]), ALU.add, [v12], [sk(cand, h0 // 4, 4096)])
                yield
            for h in range(8):
                S.op("dve", lambda e, h=h: e.max(out=tsv.ap[:, h, 0:8], in_=cand.ap[:, h, :]), [sk(cand, h, 1024)], [sk(tsv, h, 64, 0, 2)])
                if h % 4 == 3:
                    yield
            for h in range(8):
                S.op("dve", lambda e, h=h: e.max_index(out=posu.ap[:, h, 0:8], in_max=tsv.ap[:, h, 0:8], in_values=cand.ap[:, h, :]),
                     [sk(cand, h, 1024), sk(tsv, h, 64, 0, 2)], [sk(posu, h, 64, 0, 2)])
                if h % 4 == 3:
                    yield
            for h in range(8):
                S.op("dve", lambda e, h=h: e.match_replace(out=cand.ap[:, h, :], in_to_replace=tsv.ap[:, h, 0:8], in_values=cand.ap[:, h, :], imm_value=NEG),
                     [sk(cand, h, 1024), sk(tsv, h, 64, 0, 2)], [sk(cand, h, 1024)])
                if h % 4 == 3:
                    yield
            for h in range(8):
                S.op("dve", lambda e, h=h: e.max(out=tsv.ap[:, h, 8:16], in_=cand.ap[:, h, :]), [sk(cand, h, 1024)], [sk(tsv, h, 64, 1, 2)])
                if h % 4 == 3:
                    yield
            for h in range(8):
                S.op("dve", lambda e, h=h: e.max_index(out=posu.ap[:, h, 8:16], in_max=tsv.ap[:, h, 8:16], in_values=cand.ap[:, h, :]),
                     [sk(cand, h, 1024), sk(tsv, h, 64, 1, 2)], [sk(posu, h, 64, 1, 2)])
                if h % 4 == 3:
                    yield
            tt("dve", gtf.ap, tsv.ap, tsv.ap[:, :, 0:1].to_broadcast([128, 8, 16]), ALU.subtract, [tsv], [gtf])
            S.op("dve", lambda e: e.tensor_single_scalar(out=posf.ap.bitcast(U32), in_=posu.ap, scalar=15, op=ALU.bitwise_and), K(posu), K(posf))
            yield
            act(gtf.ap, gtf.ap, AF.Exp, [gtf], [gtf])
            cp("dve", pjf.ap, posf.ap.bitcast(U32), [posf], [pjf])
            yield
            S.op("dve", lambda e: e.tensor_single_scalar(out=posf.ap.bitcast(U32), in_=posu.ap, scalar=4, op=ALU.logical_shift_right), K(posu, pjf), K(posf))
            yield
            cp("dve", pif.ap, posf.ap.bitcast(U32), [posf], [pif])
            yield
            S.op("dve", lambda e: e.tensor_reduce(out=zs.ap[:, 0:8], in_=gtf.ap, axis=AX.X, op=ALU.add), K(gtf), K(zs))
            yield
            S.op("dve", lambda e: e.reciprocal(out=zs.ap[:, 0:8], in_=zs.ap[:, 0:8]), K(zs), K(zs))
            yield
            tt("dve", gtf.ap, gtf.ap, zs.ap[:, 0:8].unsqueeze(2).to_broadcast([128, 8, 16]), ALU.mult, [gtf, zs], [gtf])
            yield
            tr(psb[7].ap[:, 0:128], gtf.ap.rearrange("p h k -> p (h k)"), ident_f, [gtf, cst], [psb[7]])
            yield
            cp("act", gT.ap[:, c * 128:(c + 1) * 128], psb[7].ap[:, 0:128], [psb[7]], [gTk[c]])
            yield
            for (pp, hf, dst, dT, keys, bank) in ((pif, 0, eaf, eaT, eaTk, 1), (pjf, 1, ebf, ebT, ebTk, 2)):
                for h0 in (0, 4):
                    ohk = sk(oh, h0 // 4, 4096)
                    tt("dve", oh.ap[:, h0:h0 + 4], iota16.unsqueeze(1).unsqueeze(1).to_broadcast([128, 4, 16, 16]),
                       pp.ap[:, h0:h0 + 4].unsqueeze(3).to_broadcast([128, 4, 16, 16]), ALU.is_equal, [cst, pp], [ohk])
                    yield
                    tt("dve", oh.ap[:, h0:h0 + 4], oh.ap[:, h0:h0 + 4],
                       i12.ap[:, hf, h0:h0 + 4].unsqueeze(2).to_broadcast([128, 4, 16, 16]), ALU.mult, [ohk, i12], [ohk])
                    yield
                    S.op("dve", lambda e, dst=dst, h0=h0: e.tensor_reduce(out=dst.ap[:, h0 * 16:(h0 + 4) * 16],
                                                                     in_=oh.ap[:, h0:h0 + 4].rearrange("p h k i -> p (h k) i"), axis=AX.X, op=ALU.add),
                         [ohk], K(dst))
                    yield
                tr(psb[7].ap[:, bank * 128:(bank + 1) * 128], dst.ap, ident_f, [dst, cst], [psb[7]])
                yield
                cp("act", dT.ap[:, c * 128:(c + 1) * 128], psb[7].ap[:, bank * 128:(bank + 1) * 128], [psb[7]], [keys[c]])
                yield

        def topk_both():
            yield from topk_rest(0)
            yield from topk_rest(1)

        def ggen(b, j):
            NSB = TT // 16
            for sbk in range(NSB):
                i = sbk % 2
                tsl = slice(sbk * 16, (sbk + 1) * 16)
                tt("dve", OAb[i].ap, iota3.ap, eaT.ap[:, tsl].unsqueeze(1).to_broadcast([128, 128, 16]), ALU.is_equal,
                   [iota3, eaTk[sbk // 8]], [OAb[i]])
                tt("dve", OBb[i].ap, iota3.ap, ebT.ap[:, tsl].unsqueeze(1).to_broadcast([128, 128, 16]), ALU.is_equal,
                   [iota3, ebTk[sbk // 8]], [OBb[i]])
                tt("dve", OBb[i].ap, OBb[i].ap, gT.ap[:, tsl].unsqueeze(1).to_broadcast([128, 128, 16]), ALU.mult,
                   [OBb[i], gTk[sbk // 8]], [OBb[i]])
                for q4 in range(4):
                    bank = 4 + (sbk * 4 + q4) % 4
                    for k4 in range(4):
                        tl = q4 * 4 + k4
                        mm(psb[bank].ap[:, k4 * 128:(k4 + 1) * 128], OBb[i].ap[:, :, tl], OAb[i].ap[:, :, tl], True, True,
                           [OBb[i], OAb[i]], [psb[bank]])
                    tg = sbk * 16 + q4 * 4
                    cp("act", Gb.ap[:, tg:tg + 4, :], psb[bank].ap.rearrange("p (t a) -> p t a", t=4), [psb[bank]], [Gb])

        def dense_gen(b, j):
            t0 = j * TT
            hx, hxk, h1T, h1Tk = hxs[j % 2], hxk2[j % 2], h1Ts[j % 2], h1Tk2[j % 2]
            LA = 2
            LD = NUB - 3
            PSH = lambda a: Buf(psb[4 + a % 3].ap[:, 0:TT], psb[4 + a % 3].key)

            def ld(a):
                i = a % NUB
                dma("sp", ubuf[i].ap, u_s[a].rearrange("p (c e) -> p c e", c=8), SCR, [ubuf[i]], "d_ub%d" % i)
                dma("sp", vbuf[i].ap, v_s[a * 128:(a + 1) * 128, :], SCR, [vbuf[i]], "d_vb%d" % i)

            def hid(a):
                i = a % NUB
                if a + LD < 128:
                    ld(a + LD)
                hp = PSH(a)
                for dc in range(8):
                    mm(hp.ap, ubuf[i].ap[:, dc, :], h1T.ap[:, dc, :], dc == 0, dc == 7, [ubuf[i]] + h1Tk, [hp])

            for a in range(LD):
                ld(a)
            for a in range(min(LA, 128)):
                hid(a)
            for a in range(128):
                if a + LA < 128:
                    hid(a + LA)
                i = a % NUB
                k2 = a % 4
                hp = PSH(a)
                act(gel[k2].ap, hp.ap, AF.Gelu_apprx_tanh, [hp], [gel[k2]])
                tt("dve", actb[k2].ap, gel[k2].ap, Gb.ap[:, :, a], ALU.mult, [gel[k2], Gb], [actb[k2]])
                for c in range(2):
                    for hf in range(2):
                        bank = c * 2 + hf
                        mm(psb[bank].ap, actb[k2].ap[:, c * 128:(c + 1) * 128], vbuf[i].ap[:, hf * 512:(hf + 1) * 512],
                           a == 0, a == 127, [actb[k2], vbuf[i]], [psb[bank]])
                yield
            for c in range(2):
                for hf in range(2):
                    bank = c * 2 + hf
                    stt("dve", hx.ap[:, c, hf * 512:(hf + 1) * 512], hx.ap[:, c, hf * 512:(hf + 1) * 512], ALPHA,
                        psb[bank].ap, ALU.mult, ALU.add, [hxk[c], psb[bank]], [hxk[c]])
            run_il(ln_rows(hx, hxk, 2, 0), ln_rows(hx, hxk, 2, 1))
            dma("pool", out_d[b, t0:t0 + TT, :].rearrange("(j p) d -> p j d", p=128), hx.ap, hxk, ["outd"], "d_out")

        def out_h1(b, j):
            t0 = j * TT
            dma("pool", out_d[b, t0:t0 + TT, :].rearrange("(j p) d -> p j d", p=128), hxs[j % 2].ap, hxk2[j % 2], ["outd"], "d_out")

        for b in range(NB):
            pass1(b)
        for b in range(NB):
            S.op("dve", lambda e: e.memset(S_f.ap, 0.0), [], K(S_f))
            if dbg == 1:
                for j in range(NTI):
                    mixer_tile(b, j)
                    out_h1(b, j)
                continue
            mixer_tile(b, 0)
            peer_front(b, 0)
            if dbg == 2:
                continue
            run_il(topk_both())
            if dbg == 3:
                continue
            for j in range(NTI):
                if j + 1 < NTI:
                    mixer_tile(b, j + 1)
                    peer_front(b, j + 1)
                ggen(b, j)
                if dbg == 4:
                    continue
                if j + 2 < NTI:
                    mixer_prefetch(b, j + 2)
                gens = [dense_gen(b, j)]
                if j + 1 < NTI:
                    gens.append(topk_both())
                run_il(*gens)
        S.finish("sp")
        print("ops", S.nops, "waits", S.nwaits, "sbuf_free", nc.sbuf_bytes_remaining)
    return nc


_CACHE = {}


def _consts():
    c = np.zeros((128, 5, 128), np.float32)
    c[:, 0] = np.eye(128)
    c[:, 1] = np.triu(np.ones((128, 128)))
    c[:, 2] = np.tril(np.ones((128, 128)))
    c[:, 3] = np.arange(128)[None, :]
    c[:, 4, 0:16] = np.arange(16)[None, :]
    c[:, 4, 16] = 1.0
    return c


def _shared(inp):
    f = lambda a: np.ascontiguousarray(np.asarray(a, dtype=np.float32))
    rep = lambda a: f(np.broadcast_to(np.asarray(a, np.float32)[None], (128,) + tuple(np.shape(a))))
    u = np.asarray(inp["peer_u"], np.float32)[0].reshape(128, 128, 8, 128)
    return {
        "w_in": f(inp["w_in"][0]), "w_out": f(inp["w_out"][0]), "wq": f(inp["peer_wq"][0]),
        "k1T": f(np.asarray(inp["peer_k1"])[0].transpose(2, 0, 1)),
        "k2T": f(np.asarray(inp["peer_k2"])[0].transpose(2, 0, 1)),
        "uT": f(u.transpose(0, 3, 2, 1).reshape(128, 128, 1024)),
        "vtab": f(inp["peer_v"][0]),
        "lbl": rep(np.asarray(inp["hgrn_lb_logits"])),
        "hgn": rep(np.asarray(inp["hgrn_norm_g"])[0]),
        "glg": rep(np.asarray(inp["gmlp_ln_g"])[0].reshape(512)),
        "glb": rep(np.asarray(inp["gmlp_ln_b"])[0].reshape(512)),
        "wsT": f(np.asarray(inp["gmlp_ws"])[0].transpose(2, 0, 1)),
        "bsT": f(np.asarray(inp["gmlp_bs"])[0].T),
        "lnp": rep(np.stack([np.asarray(inp[k])[0] for k in ("ln1_g", "ln1_b", "ln2_g", "ln2_b")])),
        "cst": _consts(),
    }


def run(inp, ncores, dbg=0):
    x = np.asarray(inp["x"], np.float32)
    B, SEQ, _ = x.shape
    NB = B // ncores
    key = (NB, SEQ, dbg)
    if key not in _CACHE:
        _CACHE[key] = build(NB, SEQ, dbg)
    nc = _CACHE[key]
    sh = _shared(inp)
    maps = []
    for c in range(ncores):
        xs = np.ascontiguousarray(x[c * NB:(c + 1) * NB])
        m = dict(sh)
        m["x"] = xs
        m["xT"] = np.ascontiguousarray(xs.transpose(0, 2, 1))
        maps.append(m)
    res = run_bass_kernel_spmd(nc, maps, core_ids=list(range(ncores)))
    return np.concatenate([r["out"] for r in res.results], axis=0)


def kernel(**inputs):
    return run(inputs, 8).astype(np.float32)
```

```python
import numpy as np
from contextlib import ExitStack
import concourse.bass as bass
import concourse.mybir as mybir
from concourse.bass_utils import run_bass_kernel_spmd

F32 = mybir.dt.float32
BF16 = mybir.dt.bfloat16
U32 = mybir.dt.uint32
AF = mybir.ActivationFunctionType
ALU = mybir.AluOpType
AX = mybir.AxisListType

D = 1024
PT = 3584
NEXP = 16384
ALPHA = 2.0 ** 0.25
LN_EPS = 1e-5
RMS_EPS = 1e-6
TT = 256
NEG = -1e30


class Sync:
    def __init__(self, nc, es):
        self.nc, self.es = nc, es
        self.engs = {"pe": nc.tensor, "act": nc.scalar, "dve": nc.vector,
                     "pool": nc.gpsimd, "sp": nc.sync}
        self.sem, self.cnt, self.inc = {}, {}, {}
        self.known = {e: {} for e in self.engs}
        self.snap = {}
        self.lastw, self.readers = {}, {}
        self.arena = []
        self.nwaits = self.nops = 0
        for e in ("pe", "act", "dve", "pool"):
            self._mk(e, 1)

    def _mk(self, s, inc):
        self.sem[s] = self.es.enter_context(self.nc.semaphore("s_" + s))
        self.cnt[s] = 0
        self.inc[s] = inc


    def _split(self, lo, hi):
        out = []
        for rec in self.arena:
            if rec[1] <= lo or hi <= rec[0]:
                out.append(rec)
                continue
            cuts = [rec[0]] + [c for c in (lo, hi) if rec[0] < c < rec[1]] + [rec[1]]
            for a, b_ in zip(cuts[:-1], cuts[1:]):
                out.append([a, b_, rec[2], dict(rec[3])])
        self.arena = out
        return [rec for rec in self.arena if lo <= rec[0] and rec[1] <= hi]

    def op(self, eng, fn, reads=(), writes=(), dma=None):
        deps = {}

        def add(sq):
            if sq is not None and deps.get(sq[0], 0) < sq[1]:
                deps[sq[0]] = sq[1]

        for r in reads:
            if isinstance(r, tuple):
                for rec in self._split(r[1], r[2]):
                    add(rec[2])
            else:
                add(self.lastw.get(r))
        for w in writes:
            if isinstance(w, tuple):
                for rec in self._split(w[1], w[2]):
                    add(rec[2])
                    for s, q in rec[3].items():
                        add((s, q))
            else:
                add(self.lastw.get(w))
                for s, q in self.readers.get(w, {}).items():
                    add((s, q))
        if dma is not None:
            if dma not in self.sem:
                self._mk(dma, 16)
            if self.cnt[dma] > 0:
                add((dma, self.cnt[dma]))
        stream = dma if dma is not None else eng
        kn = self.known[eng]
        e = self.engs[eng]
        for s, q in deps.items():
            if s == "pe" and stream == "pe":
                continue
            if kn.get(s, 0) >= q:
                continue
            e.wait_ge(self.sem[s], q * self.inc[s])
            self.nwaits += 1
            for s2, q2 in self.snap[(s, q)].items():
                if kn.get(s2, 0) < q2:
                    kn[s2] = q2
            kn[s] = q
        ins = fn(e)
        self.nops += 1
        self.cnt[stream] += 1
        seq = self.cnt[stream]
        ins.then_inc(self.sem[stream], self.inc[stream])
        self.snap[(stream, seq)] = dict(kn)
        for r in reads:
            if isinstance(r, tuple):
                ins_ = sorted(self._split(r[1], r[2]), key=lambda x: x[0])
                pos = r[1]
                for rec in ins_:
                    if rec[0] > pos:
                        self.arena.append([pos, rec[0], None, {stream: seq}])
                    rec[3][stream] = seq
                    pos = rec[1]
                if pos < r[2]:
                    self.arena.append([pos, r[2], None, {stream: seq}])
            else:
                self.readers.setdefault(r, {})[stream] = seq
        for w in writes:
            if isinstance(w, tuple):
                self._split(w[1], w[2])
                self.arena = [rec for rec in self.arena
                              if not (w[1] <= rec[0] and rec[1] <= w[2])]
                self.arena.append([w[1], w[2], (stream, seq), {}])
            else:
                self.lastw[w] = (stream, seq)
                self.readers[w] = {}
        return ins

    def finish(self, eng="sp"):
        e = self.engs[eng]
        for s, c in self.cnt.items():
            if c > 0 and self.known[eng].get(s, 0) < c:
                e.wait_ge(self.sem[s], c * self.inc[s])


class Buf:
    def __init__(self, ap, key):
        self.ap, self.key = ap, key


def build(NB, SEQ, dbg=0):
    NTI = SEQ // TT
    nc = bass.Bass("TRN2", target_bir_lowering=False)
    di = lambda n, s, dt=F32: nc.dram_tensor(n, list(s), dt, kind="ExternalInput").ap()
    x_d = di("x", [NB, SEQ, D])
    xT_d = di("xT", [NB, D, SEQ])
    win_d = di("w_in", [D, PT])
    wout_d = di("w_out", [D, D])
    wq_d = di("wq", [D, 2048])
    k1T_d = di("k1T", [128, 8, 128])
    k2T_d = di("k2T", [128, 8, 128])
    uT_d = di("uT", [128, 128, 1024])
    v_d = di("vtab", [NEXP, D])
    lbl_d = di("lbl", [128, 2, 2, 512])
    hgn_d = di("hgn", [128, 512])
    glg_d = di("glg", [128, 512])
    glb_d = di("glb", [128, 512])
    wsT_d = di("wsT", [128, 4, 128])
    bsT_d = di("bsT", [128, 4])
    lnp_d = di("lnp", [128, 4, D])
    cst_d = di("cst", [128, 5, 128])
    out_d = nc.dram_tensor("out", [NB, SEQ, D], F32, kind="ExternalOutput").ap()
    dscr = lambda n, s, dt=BF16: nc.dram_tensor(n, list(s), dt, kind="Internal").ap()
    win_s = dscr("win_s", [D, PT])
    wout_s = dscr("wout_s", [D, D])
    wq_s = dscr("wq_s", [D, 2048])
    u_s = dscr("u_s", [128, 128, 1024])
    v_s = dscr("v_s", [NEXP, D])
    ck_s = dscr("ck_s", [NB * max(NTI, 1), 128, 512], F32)

    with ExitStack() as es:
        S = Sync(nc, es)
        cnt = [0]

        def sb(shape, dt, name=None):
            cnt[0] += 1
            n = "sb_" + (name or "t%d" % cnt[0])
            t = es.enter_context(nc.sbuf_tensor(n, list(shape), dt))
            return Buf(t[:], n)

        AW = 98496
        arena_t = es.enter_context(nc.sbuf_tensor("arena", [128, AW // 4], F32))

        def av(lo, nbytes, dt, pat=None, **kw):
            ap = arena_t[:, lo // 4:(lo + nbytes) // 4]
            if dt != F32:
                ap = ap.bitcast(dt)
            if pat:
                ap = ap.rearrange(pat, **kw)
            return Buf(ap, ("A", lo, lo + nbytes))

        psb = []
        for i in range(8):
            t = es.enter_context(nc.psum_tensor("ps%d" % i, [128, 512], F32))
            psb.append(Buf(t[:], "ps%d" % i))

        def psbf(i):
            return psb[i].ap.bitcast(BF16)

        def K(*bufs):
            out = []
            for b_ in bufs:
                k_ = b_.key if isinstance(b_, Buf) else b_
                if isinstance(k_, list):
                    out.extend(k_)
                else:
                    out.append(k_)
            return out


        def mm(out, lhsT, rhs, st, sp_, R, W):
            S.op("pe", lambda e: e.matmul(out, lhsT=lhsT, rhs=rhs, start=st, stop=sp_), K(*R), K(*W))

        def tr(out, in_, ident, R, W):
            S.op("pe", lambda e: e.transpose(out, in_, ident), K(*R), K(*W))

        def act(out, in_, func, R, W, **kw):
            S.op("act", lambda e: e.activation(out=out, in_=in_, func=func, **kw), K(*R), K(*W))

        def tt(eng, out, in0, in1, op, R, W):
            S.op(eng, lambda e: e.tensor_tensor(out=out, in0=in0, in1=in1, op=op), K(*R), K(*W))

        def ts(eng, out, in0, s1, s2, op0, op1, R, W):
            if s2 is None:
                S.op(eng, lambda e: e.tensor_scalar(out=out, in0=in0, scalar1=s1, scalar2=None, op0=op0), K(*R), K(*W))
            else:
                S.op(eng, lambda e: e.tensor_scalar(out=out, in0=in0, scalar1=s1, scalar2=s2, op0=op0, op1=op1), K(*R), K(*W))

        def stt(eng, out, in0, sc, in1, op0, op1, R, W):
            S.op(eng, lambda e: e.scalar_tensor_tensor(out=out, in0=in0, scalar=sc, in1=in1, op0=op0, op1=op1), K(*R), K(*W))

        def cp(eng, out, in_, R, W):
            if eng == "act":
                S.op("act", lambda e: e.copy(out=out, in_=in_), K(*R), K(*W))
            else:
                S.op(eng, lambda e: e.tensor_copy(out=out, in_=in_), K(*R), K(*W))

        def rsq(out, in_, scale, eps, R, W):
            act(out, in_, AF.Sqrt, R, W, scale=scale, bias=eps)
            S.op("dve", lambda e: e.reciprocal(out=out, in_=out), K(*W), K(*W))

        def dma(q, out, in_, R, W, stream):
            S.op(q, lambda e: e.dma_start(out=out, in_=in_), K(*R), K(*W), dma=stream)

        cst = sb([128, 5, 128], F32, "cst")
        dma("sp", cst.ap, cst_d, [], [cst], "d_cst")
        ident_f = cst.ap[:, 0, :]
        triU = cst.ap[:, 1, :]
        triL = cst.ap[:, 2, :]
        iota16 = cst.ap[:, 4, 0:16]
        ones_c = cst.ap[:, 4, 16:17]
        cstb = sb([128, 5, 128], BF16, "cstb")
        dma("pool", cstb.ap, cst_d, [], [cstb], "d_cstb")
        ident_b = cstb.ap[:, 0, :]
        iota_b = cstb.ap[:, 3, :]
        iota3 = sb([128, 128, 16], BF16, "iota3")
        cp("dve", iota3.ap, iota_b.unsqueeze(2).to_broadcast([128, 128, 16]), [cstb], [iota3])
        lbl = av(32768, 8192, F32, "p (a b c) -> p a b c", a=2, b=2)
        dma("sp", lbl.ap, lbl_d, [], [lbl], "d_lbl")
        lb = sb([128, 2, 512], F32, "lb")
        oml = sb([128, 2, 512], F32, "oml")
        tt("dve", lb.ap, lbl.ap[:, :, 0, :], lbl.ap[:, :, 1, :], ALU.subtract, [lbl], [lb])
        act(lb.ap, lb.ap, AF.Sigmoid, [lb], [lb])
        ts("dve", oml.ap, lb.ap, -1.0, 1.0, ALU.mult, ALU.add, [lb], [oml])
        hgn = sb([128, 512], F32, "hgn")
        glg = sb([128, 512], F32, "glg")
        glb = sb([128, 512], F32, "glb")
        bsT = sb([128, 4], F32, "bsT")
        lnp = sb([128, 4, D], F32, "lnp")
        dma("sp", hgn.ap, hgn_d, [], [hgn], "d_c1")
        dma("sp", glg.ap, glg_d, [], [glg], "d_c2")
        dma("sp", glb.ap, glb_d, [], [glb], "d_c3")
        dma("sp", bsT.ap, bsT_d, [], [bsT], "d_c4")
        dma("sp", lnp.ap, lnp_d, [], [lnp], "d_c5")
        wsT = sb([128, 4, 128], BF16, "wsT")
        k1T = sb([128, 8, 128], BF16, "k1T")
        k2T = sb([128, 8, 128], BF16, "k2T")
        dma("pool", wsT.ap, wsT_d, [], [wsT], "d_c6")
        dma("pool", k1T.ap, k1T_d, [], [k1T], "d_c7")
        dma("pool", k2T.ap, k2T_d, [], [k2T], "d_c8")

        SCR = ["scr0", "scr1", "scr2", "scr3", "scr4", "scr5", "scr6", "scr7"]
        stg = [av(i * 4096, 4096, BF16) for i in range(4)]
        nst = [0]

        def cast_copy(src, dst):
            i = nst[0] % 4
            nst[0] += 1
            F_ = src.shape[1]
            dma("pool", stg[i].ap[:, 0:F_], src, [], [stg[i]], "d_stg%d" % i)
            dma("sp", dst, stg[i].ap[:, 0:F_], [stg[i]], ["scr%d" % i], "d_sto%d" % i)

        for c in range(8):
            for (c0, c1) in ((0, 2048), (2048, PT)):
                cast_copy(win_d[c * 128:(c + 1) * 128, c0:c1], win_s[c * 128:(c + 1) * 128, c0:c1])
            cast_copy(wout_d[c * 128:(c + 1) * 128, :], wout_s[c * 128:(c + 1) * 128, :])
            cast_copy(wq_d[c * 128:(c + 1) * 128, :], wq_s[c * 128:(c + 1) * 128, :])
        stg2 = [av(o_, 4096, BF16, "p (a f) -> p a f", a=2) for o_ in (65728, 69824, 90304, 94400)]
        nst2 = [0]

        def cast_copy2(src, dst):
            i = nst2[0] % 4
            nst2[0] += 1
            dma("pool", stg2[i].ap, src, [], [stg2[i]], "d_stg2%d" % i)
            dma("sp", dst, stg2[i].ap, [stg2[i]], ["scr%d" % (4 + i)], "d_sto2%d" % i)

        tab_next = [0]

        def copy_tables(n):
            n = 2 * ((n + 1) // 2)
            for a in range(tab_next[0], min(128, tab_next[0] + n), 2):
                cast_copy2(uT_d[a:a + 2].rearrange("a p f -> p a f"), u_s[a:a + 2].rearrange("a p f -> p a f"))
                cast_copy2(v_d[a * 128:(a + 2) * 128, :].rearrange("(a b) f -> b a f", b=128),
                           v_s[a * 128:(a + 2) * 128, :].rearrange("(a b) f -> b a f", b=128))
            tab_next[0] = min(128, tab_next[0] + n)


        S_f = sb([128, 512], F32, "S_f")
        S_b = sb([128, 512], F32, "S_b")
        Sbf = {k: sb([128, 512], BF16, "Sbf_%s" % k) for k in ("f0", "f1", "b0", "b1")}
        hxs = [sb([128, 2, D], F32, "hx%d" % i) for i in range(2)]
        xT = [sb([128, 8, TT], BF16, "xT%d" % i) for i in range(2)]
        h1b = [av(69824 + i * 2048, 2048, BF16) for i in range(2)]
        hxk2 = [["hx%d_c0" % i, "hx%d_c1" % i] for i in range(2)]
        h1Tk2 = [["h1T%d_c0" % i, "h1T%d_c1" % i] for i in range(2)]
        lnst = [dict(st=sb([128, 4, 6], F32, "lst%d" % i), mv=sb([128, 4, 2], F32, "lmv%d" % i), rs=sb([128, 4], F32, "lrs%d" % i), nm=sb([128, 4], F32, "lnm%d" % i)) for i in range(2)]
        h1Ts = [sb([128, 8, TT], BF16, "h1T%d" % i) for i in range(2)]
        yT = [av(65728 + i * 2048, 2048, BF16, "p (c t) -> p c t", c=8) for i in range(2)]
        eaT = sb([128, TT], BF16, "eaT")
        ebT = sb([128, TT], BF16, "ebT")
        gT = sb([128, TT], BF16, "gT")
        NUB = 5
        ubuf = [sb([128, 8, 128], BF16, "ub%d" % i) for i in range(NUB)]
        vbuf = [sb([128, D], BF16, "vb%d" % i) for i in range(NUB)]
        st8 = sb([128, 2, 6], F32, "st8")
        mv = sb([128, 2], F32, "mv")
        rstd = sb([128, 1], F32, "rstd")
        nmr = sb([128, 1], F32, "nmr")

        wsl = [av(73920 + i * 8192, 8192, BF16, "p (c e) -> p c e", c=8) for i in range(3)]
        nws = [0]

        def load_w(src_s, c0, slot=None):
            if slot is None:
                i = nws[0] % 3
                nws[0] += 1
            else:
                i = slot
            dma("sp", wsl[i].ap, src_s[:, c0:c0 + 512].rearrange("(c p) e -> p c e", p=128),
                SCR, [wsl[i]], "d_w%d" % i)
            return wsl[i]

        MB = 0
        off = [MB]

        def mt(nbytes, dt, pat=None, **kw):
            b = av(off[0], nbytes, dt, pat, **kw)
            off[0] += nbytes
            return b

        trn = {}
        for c in range(2):
            for d_ in range(2):
                trn[(c, d_)] = dict(sf=mt(2048, F32), logf=mt(2048, F32), E=mt(2048, F32), qt=mt(1024, BF16))
        pcd = {}
        for c in range(2):
            for d_ in range(2):
                pcd[(c, d_)] = dict(qtT=mt(1024, BF16, "p (h t) -> p h t", h=4),
                                    ktT=mt(1024, BF16, "p (h t) -> p h t", h=4),
                                    ktk=mt(1024, BF16), scm=mt(1024, BF16, "p (h t) -> p h t", h=4),
                                    dec=mt(32, F32))
        pc = [dict(q=mt(2048, F32), vtok=mt(1024, BF16), sg=mt(2048, F32), gu=mt(2048, F32),
                   vc=mt(1024, BF16), y=mt(2048, BF16), ss=mt(32, F32),
                   gv=trn[(c, 0)]["sf"], osb=trn[(c, 0)]["logf"], sq=trn[(c, 0)]["E"]) for c in range(2)]
        assert off[0] <= 65728, off[0]
        qT = av(0, 8192, BF16, "p (c t) -> p c t", c=16)
        oh = av(24576, 8192, F32, "p (h k i) -> p h k i", h=8, k=16)
        TK = []
        for c_, (b0, sm) in enumerate(((8192, 32768), (40960, 57344))):
            TK.append(dict(
                s12=av(b0, 8192, F32, "p (a h k) -> p a h k", a=2, h=8),
                cand=av(b0 + 8192, 8192, F32, "p (h k) -> p h k", h=8),
                v12=av(sm, 1024, F32, "p (a h k) -> p a h k", a=2, h=8),
                i12u=av(sm + 1024, 1024, U32, "p (a h k) -> p a h k", a=2, h=8),
                i12=av(sm + 2048, 1024, F32, "p (a h k) -> p a h k", a=2, h=8),
                tsv=av(sm + 3072, 512, F32, "p (h k) -> p h k", h=8),
                posu=av(sm + 3584, 512, U32, "p (h k) -> p h k", h=8),
                posf=av(sm + 4096, 512, F32, "p (h k) -> p h k", h=8),
                pjf=av(sm + 4608, 512, F32, "p (h k) -> p h k", h=8),
                pif=av(sm + 5120, 512, F32, "p (h k) -> p h k", h=8),
                eaf=av(sm + 5632, 512, F32),
                ebf=av(sm + 6144, 512, F32),
                gtf=av(sm + 6656, 512, F32, "p (h k) -> p h k", h=8),
                zs=av(sm + 7168, 32, F32)))
        TK[0]["cand"] = av(16384, 8192, F32, "p (h k) -> p h k", h=8)
        TK[1]["s12"] = av(40960, 8192, F32, "p (a h k) -> p a h k", a=2, h=8)
        TK[1]["cand"] = av(49152, 8192, F32, "p (h k) -> p h k", h=8)
        Gb = av(0, 65536, BF16, "p (t a) -> p t a", a=128)
        OAb = [av(81920 + i * 4096, 4096, BF16, "p (a t) -> p a t", t=16) for i in range(2)]
        OBb = [av(90112 + i * 4096, 4096, BF16, "p (a t) -> p a t", t=16) for i in range(2)]
        gel = [sb([128, TT], F32, "gel%d" % i) for i in range(4)]
        actb = [sb([128, TT], BF16, "actb%d" % i) for i in range(4)]

        def ln_rows(z, zkeys, gi, c):
            zz = z.ap[:, c, :]
            zk = zkeys[c]
            st = lnst[c]
            for hf in range(2):
                S.op("dve", lambda e, hf=hf: e.bn_stats(out=st["st"].ap[:, hf, :], in_=zz[:, hf * 512:(hf + 1) * 512]),
                     K(zk), K(st["st"]))
            yield
            S.op("dve", lambda e: e.bn_aggr(out=st["mv"].ap[:, 0, :], in_=st["st"].ap[:, 0:2, :].rearrange("p a b -> p (a b)")), K(st["st"]), K(st["mv"]))
            yield
            act(st["rs"].ap[:, 0:1], st["mv"].ap[:, 0, 1:2], AF.Sqrt, [st["mv"]], [st["rs"]], scale=1.0, bias=LN_EPS)
            yield
            S.op("dve", lambda e: e.reciprocal(out=st["rs"].ap[:, 0:1], in_=st["rs"].ap[:, 0:1]), K(st["rs"]), K(st["rs"]))
            yield
            ts("dve", st["nm"].ap[:, 0:1], st["mv"].ap[:, 0, 0:1], -1.0, st["rs"].ap[:, 0:1], ALU.mult, ALU.mult, [st["mv"], st["rs"]], [st["nm"]])
            yield
            act(zz, zz, AF.Identity, [zk, st["nm"], st["rs"]], [zk], scale=st["rs"].ap[:, 0:1], bias=st["nm"].ap[:, 0:1])
            yield
            tt("dve", zz, zz, lnp.ap[:, gi, :], ALU.mult, [zk, lnp], [zk])
            yield
            tt("dve", zz, zz, lnp.ap[:, gi + 1, :], ALU.add, [zk, lnp], [zk])
            yield

        def proj(xt, c, wslot, bank):
            for dc in range(8):
                mm(psb[bank].ap, xt.ap[:, dc, c * 128:(c + 1) * 128], wslot.ap[:, dc, :], dc == 0, dc == 7,
                   [xt, wslot], [psb[bank]])

        def run_il(*gens):
            gens = list(gens)
            while gens:
                for g_ in list(gens):
                    try:
                        next(g_)
                    except StopIteration:
                        gens.remove(g_)

        def gate_chain(xt, wslot, c, d_, zc, tb, toff, need_q, qsrc, bkey=None):
            bkey = bkey or (c, d_)
            T_ = trn[bkey]
            P_ = pcd[bkey]
            Z = psb[zc]
            proj(xt, c, wslot, zc)
            yield
            act(T_["sf"].ap, Z.ap, AF.Sigmoid, [Z], [T_["sf"]])
            yield
            tt("dve", T_["sf"].ap, T_["sf"].ap, oml.ap[:, d_, :], ALU.mult, [T_["sf"], oml], [T_["sf"]])
            yield
            tt("dve", T_["sf"].ap, T_["sf"].ap, lb.ap[:, d_, :], ALU.add, [T_["sf"], lb], [T_["sf"]])
            yield
            act(T_["logf"].ap, T_["sf"].ap, AF.Ln, [T_["sf"]], [T_["logf"]])
            yield
            ts("dve", T_["sf"].ap, T_["sf"].ap, -1.0, 1.0, ALU.mult, ALU.add, [T_["sf"]], [T_["sf"]])
            tri = triU if d_ == 0 else triL
            mm(Z.ap, tri, T_["logf"].ap, True, True, [cst, T_["logf"]], [Z])
            q4 = (2 * bkey[0] + bkey[1]) * 4
            for h in range(4):
                mm(psb[6].ap[:, q4 + h:q4 + h + 1], T_["logf"].ap[:, h * 128:(h + 1) * 128], ones_c, True, True,
                   [T_["logf"], cst], [psb[6]])
            yield
            act(T_["E"].ap, Z.ap, AF.Exp, [Z], [T_["E"]], scale=-1.0)
            act(P_["dec"].ap[:, 0:4], psb[6].ap[:, q4:q4 + 4], AF.Exp, [psb[6]], [P_["dec"]])
            yield
            tt("dve", P_["ktk"].ap, T_["sf"].ap, T_["E"].ap, ALU.mult, [T_["sf"], T_["E"]], [P_["ktk"]])
            yield
            if need_q:
                act(T_["E"].ap, Z.ap, AF.Exp, [Z, P_["ktk"]], [T_["E"]])
                pb = psbf(tb)
                for h in range(4):
                    tr(pb[:, toff + h * 128:toff + (h + 1) * 128], P_["ktk"].ap[:, h * 128:(h + 1) * 128], ident_b,
                       [P_["ktk"], cstb], [psb[tb]])
                yield
                cp("act", P_["ktT"].ap, pb[:, toff:toff + 512].rearrange("p (h t) -> p h t", h=4), [psb[tb]], [P_["ktT"]])
                tt("dve", T_["qt"].ap, qsrc.ap, T_["E"].ap, ALU.mult, [qsrc, T_["E"]], [T_["qt"]])
                yield
                for h in range(4):
                    tr(pb[:, toff + h * 128:toff + (h + 1) * 128], T_["qt"].ap[:, h * 128:(h + 1) * 128], ident_b,
                       [T_["qt"], cstb], [psb[tb]])
                yield
                cp("act", P_["qtT"].ap, pb[:, toff:toff + 512].rearrange("p (h t) -> p h t", h=4), [psb[tb]], [P_["qtT"]])
                yield

        def state_update(Sst, c, d_, vtok, bank, snap_to=None, bkey=None):
            P_ = pcd[bkey or (c, d_)]
            if snap_to is not None:
                cp("act", snap_to.ap, Sst.ap, [Sst], [snap_to])
            for h in range(4):
                mm(psb[bank].ap[:, h * 128:(h + 1) * 128], P_["ktk"].ap[:, h * 128:(h + 1) * 128],
                   vtok.ap[:, h * 128:(h + 1) * 128], True, True, [P_["ktk"], vtok], [psb[bank]])
            yield
            tt("dve", Sst.ap, Sst.ap, psb[bank].ap, ALU.add, [Sst, psb[bank]], [Sst])
            yield
            for h in range(4):
                act(Sst.ap[:, h * 128:(h + 1) * 128], Sst.ap[:, h * 128:(h + 1) * 128], AF.Copy, [Sst, P_["dec"]], [Sst],
                    scale=P_["dec"].ap[:, h:h + 1])
            yield

        nx = [0]

        def load_xT(b, j):
            i = nx[0] % 2
            nx[0] += 1
            dma("pool", xT[i].ap, xT_d[b, :, j * TT:(j + 1) * TT].rearrange("(c p) t -> p c t", p=128),
                [], [xT[i]], "d_xT%d" % i)
            return xT[i]

        def pass1(b):
            S.op("dve", lambda e: e.memset(S_b.ap, 0.0), [], K(S_b))
            if NTI < 2:
                dma("sp", ck_s[b * NTI], S_b.ap, [S_b], ["ck%d_0" % b], "d_cko")
                copy_tables(128)
                return
            wzb = load_w(win_s, 1024, slot=0)
            wi = load_w(win_s, 1536, slot=1)
            nper = -(-128 // max(NB * (NTI - 1), 1))

            def chains(j, st):
                xt = load_xT(b, j)
                vt = [pc[c]["vtok"] if st == 1 else pc[c]["vc"] for c in range(2)]

                def vchain(c):
                    proj(xt, c, wi, 4 + c)
                    yield
                    cp("act", vt[c].ap, psb[4 + c].ap, [psb[4 + c]], [vt[c]])
                    yield

                zb0 = 0 if st == 1 else 2
                return [gate_chain(xt, wzb, 1, 1, zb0, 4, 0, False, None, bkey=(1, st)),
                        gate_chain(xt, wzb, 0, 1, zb0 + 1, 5, 0, False, None, bkey=(0, st)),
                        vchain(1), vchain(0)]

            def updates(j, st):
                vt = [pc[c]["vtok"] if st == 1 else pc[c]["vc"] for c in range(2)]
                dma("sp", ck_s[b * NTI + j], S_b.ap, [S_b], ["ck%d_%d" % (b, j)], "d_cko")
                for c in (1, 0):
                    yield from state_update(S_b, c, 1, vt[c], 7, bkey=(c, st))

            st = 1
            run_il(*chains(NTI - 1, st))
            for j in range(NTI - 1, 0, -1):
                nxt = chains(j - 1, 1 - st) if j - 1 >= 1 else []
                run_il(updates(j, st), *nxt)
                st = 1 - st
                copy_tables(nper)
            dma("sp", ck_s[b * NTI], S_b.ap, [S_b], ["ck%d_0" % b], "d_cko")
            if b == NB - 1:
                copy_tables(128)

        pre = {}
        prewq = {}

        def mixer_prefetch(b, j):
            xt = load_xT(b, j)
            dma("sp", S_b.ap, ck_s[b * NTI + j], ["ck%d_%d" % (b, j)], [S_b], "d_cki")
            pre[(b, j)] = xt

        def mixer_tile(b, j):
            t0 = j * TT
            hx, hxk, h1T, h1Tk = hxs[j % 2], hxk2[j % 2], h1Ts[j % 2], h1Tk2[j % 2]
            if (b, j) not in pre:
                mixer_prefetch(b, j)
            xt = pre.pop((b, j))
            wq_, wi_, wg_ = load_w(win_s, 0), load_w(win_s, 3 * 512), load_w(win_s, 4 * 512)

            def evac(c, w, bank, kind):
                proj(xt, c, w, bank)
                pb = psb[bank]
                yield
                if kind == "q":
                    cp("act", pc[c]["q"].ap, pb.ap, [pb], [pc[c]["q"]])
                elif kind == "i":
                    cp("act", pc[c]["vtok"].ap, pb.ap, [pb], [pc[c]["vtok"]])
                elif kind == "g":
                    act(pc[c]["sg"].ap, pb.ap, AF.Silu, [pb], [pc[c]["sg"]])
                    yield
                    tt("dve", pc[c]["sg"].ap, pc[c]["sg"].ap, hgn.ap, ALU.mult, [pc[c]["sg"], hgn], [pc[c]["sg"]])
                elif kind == "u":
                    act(pc[c]["gu"].ap, pb.ap, AF.Gelu_apprx_tanh, [pb], [pc[c]["gu"]])
                else:
                    gv = pc[c]["gv"]
                    act(gv.ap, pb.ap, AF.Gelu_apprx_tanh, [pb], [gv])
                    yield
                    st = lnst[c]
                    for gg in range(4):
                        sl = gv.ap[:, gg * 128:(gg + 1) * 128]
                        S.op("dve", lambda e, sl=sl, gg=gg: e.bn_stats(out=st["st"].ap[:, gg, :], in_=sl), K(gv), K(st["st"]))
                    yield
                    for gg in range(4):
                        S.op("dve", lambda e, gg=gg: e.bn_aggr(out=st["mv"].ap[:, gg, :], in_=st["st"].ap[:, gg, :]), K(st["st"]), K(st["mv"]))
                    yield
                    act(st["rs"].ap, st["mv"].ap[:, :, 1], AF.Sqrt, [st["mv"]], [st["rs"]], scale=1.0, bias=LN_EPS)
                    yield
                    S.op("dve", lambda e: e.reciprocal(out=st["rs"].ap, in_=st["rs"].ap), K(st["rs"]), K(st["rs"]))
                    yield
                    stt("dve", st["nm"].ap, st["mv"].ap[:, :, 0], -1.0, st["rs"].ap, ALU.mult, ALU.mult, [st["mv"], st["rs"]], [st["nm"]])
                    yield
                    for gg in range(4):
                        sl = gv.ap[:, gg * 128:(gg + 1) * 128]
                        act(sl, sl, AF.Identity, [gv, st["nm"], st["rs"]], [gv], scale=st["rs"].ap[:, gg:gg + 1], bias=st["nm"].ap[:, gg:gg + 1])
                    yield
                    tt("dve", gv.ap, gv.ap, glg.ap, ALU.mult, [gv, glg], [gv])
                    yield
                    tt("dve", pc[c]["vc"].ap, gv.ap, glb.ap, ALU.add, [gv, glb], [pc[c]["vc"]])
                yield

            run_il(evac(0, wq_, 0, "q"), evac(1, wq_, 1, "q"), evac(0, wi_, 2, "i"), evac(1, wi_, 3, "i"),
                   evac(0, wg_, 4, "g"), evac(1, wg_, 5, "g"))
            wzf = load_w(win_s, 512)
            wzb = load_w(win_s, 1024)
            dma("sp", hx.ap, x_d[b, t0:t0 + TT, :].rearrange("(j p) d -> p j d", p=128), [], hxk, "d_hx")
            run_il(gate_chain(xt, wzf, 0, 0, 0, 4, 0, True, pc[0]["q"]), gate_chain(xt, wzf, 1, 0, 1, 5, 0, True, pc[1]["q"]),
                   gate_chain(xt, wzb, 0, 1, 2, 4, 512, True, pc[0]["q"]), gate_chain(xt, wzb, 1, 1, 3, 5, 512, True, pc[1]["q"]))
            wu_ = load_w(win_s, 5 * 512)
            wv_ = load_w(win_s, 6 * 512)

            def fwd_chain():
                yield from state_update(S_f, 0, 0, pc[0]["vtok"], 4, Sbf["f0"])
                yield from state_update(S_f, 1, 0, pc[1]["vtok"], 4, Sbf["f1"])

            def bwd_chain():
                yield from state_update(S_b, 1, 1, pc[1]["vtok"], 5, Sbf["b1"])
                cp("act", Sbf["b0"].ap, S_b.ap, [S_b], [Sbf["b0"]])
                yield

            run_il(evac(0, wu_, 0, "u"), evac(1, wu_, 1, "u"), evac(0, wv_, 2, "v"), evac(1, wv_, 3, "v"),
                   fwd_chain(), bwd_chain())
            wo = [load_w(wout_s, 0), load_w(wout_s, 512)]
            prewq[(b, j)] = load_w(wq_s, 0)

            def part2(c):
                P = pc[c]
                B0 = 4 * c
                for d_ in range(2):
                    Q = pcd[(c, d_)]
                    bk = psb[B0 + d_]
                    for h in range(4):
                        mm(bk.ap[:, h * 128:(h + 1) * 128], Q["ktT"].ap[:, h, :], Q["qtT"].ap[:, h, :], True, True,
                           [Q["ktT"], Q["qtT"]], [bk])
                    yield
                    tri = triU if d_ == 0 else triL
                    tt("dve", Q["scm"].ap, bk.ap.rearrange("p (h t) -> p h t", h=4),
                       tri.unsqueeze(1).to_broadcast([128, 4, 128]), ALU.mult, [bk, cst], [Q["scm"]])
                    yield
                ob = psb[B0 + 2]
                for h in range(4):
                    o_ = ob.ap[:, h * 128:(h + 1) * 128]
                    vs = P["vtok"].ap[:, h * 128:(h + 1) * 128]
                    Qf, Qb = pcd[(c, 0)], pcd[(c, 1)]
                    sf_, sb_ = Sbf["f%d" % c], Sbf["b%d" % c]
                    mm(o_, Qf["scm"].ap[:, h, :], vs, True, False, [Qf["scm"], P["vtok"]], [ob])
                    mm(o_, Qf["qtT"].ap[:, h, :], sf_.ap[:, h * 128:(h + 1) * 128], False, False, [Qf["qtT"], sf_], [ob])
                    mm(o_, Qb["scm"].ap[:, h, :], vs, False, False, [Qb["scm"], P["vtok"]], [ob])
                    mm(o_, Qb["qtT"].ap[:, h, :], sb_.ap[:, h * 128:(h + 1) * 128], False, True, [Qb["qtT"], sb_], [ob])
                gb_ = psb[B0 + 3]
                for gg in range(4):
                    mm(gb_.ap[:, gg * 128:(gg + 1) * 128], wsT.ap[:, gg, :], P["vc"].ap[:, gg * 128:(gg + 1) * 128],
                       True, True, [wsT, P["vc"]], [gb_])
                yield
                cp("act", P["osb"].ap, ob.ap, [ob], [P["osb"]])
                for gg in range(4):
                    stt("dve", P["y"].ap[:, 512 + gg * 128:512 + (gg + 1) * 128], gb_.ap[:, gg * 128:(gg + 1) * 128],
                        bsT.ap[:, gg:gg + 1], P["gu"].ap[:, gg * 128:(gg + 1) * 128], ALU.add, ALU.mult,
                        [gb_, bsT, P["gu"]], [P["y"]])
                yield
                sq = P["sq"]
                tt("dve", sq.ap, P["osb"].ap, P["osb"].ap, ALU.mult, [P["osb"]], [sq])
                yield
                S.op("dve", lambda e: e.tensor_reduce(out=P["ss"].ap[:, 0:4], in_=sq.ap.rearrange("p (h e) -> p h e", h=4),
                                                      axis=AX.X, op=ALU.add), K(sq), K(P["ss"]))
                yield
                act(P["ss"].ap[:, 0:4], P["ss"].ap[:, 0:4], AF.Sqrt, [P["ss"]], [P["ss"]], scale=1.0 / 128, bias=RMS_EPS)
                tt("dve", P["osb"].ap, P["osb"].ap, P["sg"].ap, ALU.mult, [P["osb"], P["sg"]], [P["osb"]])
                yield
                S.op("dve", lambda e: e.reciprocal(out=P["ss"].ap[:, 0:4], in_=P["ss"].ap[:, 0:4]), K(P["ss"]), K(P["ss"]))
                yield
                for h in range(4):
                    ts("dve", P["y"].ap[:, h * 128:(h + 1) * 128], P["osb"].ap[:, h * 128:(h + 1) * 128],
                       P["ss"].ap[:, h:h + 1], None, ALU.mult, None, [P["osb"], P["ss"]], [P["y"]])
                yield
                tb_ = psb[B0]
                pb = psbf(B0)
                for ec in range(8):
                    tr(pb[:, ec * 128:(ec + 1) * 128], P["y"].ap[:, ec * 128:(ec + 1) * 128], ident_b, [P["y"], cstb], [tb_])
                yield
                cp("act", yT[c].ap, pb[:, 0:1024].rearrange("p (c t) -> p c t", c=8), [tb_], [yT[c]])
                yield
                for hf in range(2):
                    bk = psb[B0 + 1 + hf]
                    for ec in range(8):
                        mm(bk.ap, yT[c].ap[:, ec, :], wo[hf].ap[:, ec, :], ec == 0, ec == 7, [yT[c], wo[hf]], [bk])
                yield
                for hf in range(2):
                    bk = psb[B0 + 1 + hf]
                    stt("dve", hx.ap[:, c, hf * 512:(hf + 1) * 512], hx.ap[:, c, hf * 512:(hf + 1) * 512], ALPHA,
                        bk.ap, ALU.mult, ALU.add, [hxk[c], bk], [hxk[c]])
                yield
                yield from ln_rows(hx, hxk, 0, c)
                cp("act", h1b[c].ap, hx.ap[:, c, :], [hxk[c]], [h1b[c]])
                yield
                pb = psbf(B0 + 3)
                for dc in range(8):
                    tr(pb[:, dc * 128:(dc + 1) * 128], h1b[c].ap[:, dc * 128:(dc + 1) * 128], ident_b, [h1b[c], cstb], [psb[B0 + 3]])
                yield
                cp("act", h1T.ap[:, :, c * 128:(c + 1) * 128], pb[:, 0:1024].rearrange("p (c t) -> p c t", c=8),
                   [psb[B0 + 3]], [h1Tk[c]])
                yield

            run_il(part2(0), part2(1))

        qT = av(0, 8192, BF16, "p (c t) -> p c t", c=16)
        s12c = [av(65536 + c_ * 8192, 8192, F32, "p (a h k) -> p a h k", a=2, h=8) for c_ in range(2)]
        cand = av(81920, 8192, F32, "p (h k) -> p h k", h=8)
        oh = av(81920, 8192, F32, "p (h k i) -> p h k i", h=8, k=16)
        sm = 90112
        v12 = av(sm, 1024, F32, "p (a h k) -> p a h k", a=2, h=8)
        i12u = av(sm + 1024, 1024, U32, "p (a h k) -> p a h k", a=2, h=8)
        i12 = av(sm + 2048, 1024, F32, "p (a h k) -> p a h k", a=2, h=8)
        tsv = av(sm + 3072, 512, F32, "p (h k) -> p h k", h=8)
        posu = av(sm + 3584, 512, U32, "p (h k) -> p h k", h=8)
        posf = av(sm + 4096, 512, F32, "p (h k) -> p h k", h=8)
        pjf = av(sm + 4608, 512, F32, "p (h k) -> p h k", h=8)
        pif = av(sm + 5120, 512, F32, "p (h k) -> p h k", h=8)
        eaf = av(sm + 5632, 512, F32)
        ebf = av(sm + 6144, 512, F32)
        gtf = av(sm + 6656, 512, F32, "p (h k) -> p h k", h=8)
        zs = av(sm + 7168, 32, F32)
        sk = lambda buf, idx, n, part=0, np_=1: ("A", buf.key[1] + idx * n + part * (n // np_), buf.key[1] + idx * n + (part + 1) * (n // np_))
        HH = [(hf, h) for hf in range(2) for h in range(8)]
        eaTk = ["eaT0", "eaT1"]
        ebTk = ["ebT0", "ebT1"]
        gTk = ["gT0", "gT1"]

        def peer_front(b, j):
            h1T, h1Tk = h1Ts[j % 2], h1Tk2[j % 2]
            for g in range(4):
                w = prewq.pop((b, j), None) if g == 0 else None
                if w is None:
                    w = load_w(wq_s, g * 512)
                for cc in range(4):
                    ce = g * 4 + cc
                    bank = ce % 2
                    for dc in range(8):
                        mm(psb[bank].ap[:, 0:TT], w.ap[:, dc, cc * 128:(cc + 1) * 128], h1T.ap[:, dc, :], dc == 0, dc == 7,
                           [w] + h1Tk, [psb[bank]])
                    cp("act", qT.ap[:, ce, :], psb[bank].ap[:, 0:TT], [psb[bank]], [qT])
            for c in range(2):
                PB = 4 * c
                for hf in range(2):
                    kT = k1T if hf == 0 else k2T
                    for h in range(8):
                        bank = PB + hf * 2 + h // 4
                        mm(psb[bank].ap[:, (h % 4) * 128:(h % 4 + 1) * 128], qT.ap[:, 2 * h + hf, c * 128:(c + 1) * 128],
                           kT.ap[:, h, :], True, True, [qT, kT], [psb[bank]])
                for hf in range(2):
                    for q4 in range(2):
                        bank = PB + hf * 2 + q4
                        cp("act", s12c[c].ap[:, hf, q4 * 4:(q4 + 1) * 4, :], psb[bank].ap.rearrange("p (h k) -> p h k", h=4),
                           [psb[bank]], [sk(s12c[c], hf * 2 + q4, 2048)])

        def topk_rest(c):
            s12 = s12c[c]
            for (hf, h) in HH:
                q = hf * 8 + h
                S.op("dve", lambda e, hf=hf, h=h: e.max(out=v12.ap[:, hf, h, 0:8], in_=s12.ap[:, hf, h, :]),
                     [sk(s12, q, 512)], [sk(v12, q, 64, 0, 2)])
                if q % 4 == 3:
                    yield
            for (hf, h) in HH:
                q = hf * 8 + h
                S.op("dve", lambda e, hf=hf, h=h: e.max_index(out=i12u.ap[:, hf, h, 0:8], in_max=v12.ap[:, hf, h, 0:8], in_values=s12.ap[:, hf, h, :]),
                     [sk(s12, q, 512), sk(v12, q, 64, 0, 2)], [sk(i12u, q, 64, 0, 2)])
                if q % 4 == 3:
                    yield
            for (hf, h) in HH:
                q = hf * 8 + h
                S.op("dve", lambda e, hf=hf, h=h: e.match_replace(out=s12.ap[:, hf, h, :], in_to_replace=v12.ap[:, hf, h, 0:8], in_values=s12.ap[:, hf, h, :], imm_value=NEG),
                     [sk(s12, q, 512), sk(v12, q, 64, 0, 2)], [sk(s12, q, 512)])
                if q % 4 == 3:
                    yield
            for (hf, h) in HH:
                q = hf * 8 + h
                S.op("dve", lambda e, hf=hf, h=h: e.max(out=v12.ap[:, hf, h, 8:16], in_=s12.ap[:, hf, h, :]),
                     [sk(s12, q, 512)], [sk(v12, q, 64, 1, 2)])
                if q % 4 == 3:
                    yield
            for (hf, h) in HH:
                q = hf * 8 + h
                S.op("dve", lambda e, hf=hf, h=h: e.max_index(out=i12u.ap[:, hf, h, 8:16], in_max=v12.ap[:, hf, h, 8:16], in_values=s12.ap[:, hf, h, :]),
                     [sk(s12, q, 512), sk(v12, q, 64, 1, 2)], [sk(i12u, q, 64, 1, 2)])
                if q % 4 == 3:
                    yield
            cp("dve", i12.ap, i12u.ap, [i12u], [i12])
            yield
            for h0 in (0, 4):
                tt("dve", cand.ap[:, h0:h0 + 4].rearrange("p h (i j) -> p h i j", i=16),
                   v12.ap[:, 0, h0:h0 + 4].unsqueeze(3).to_broadcast([128, 4, 16, 16]),
                   v12.ap[:, 1, h0:h0 + 4].unsqueeze(2).to_broadcast([128, 4, 16, 16]), ALU.add, [v12], [sk(cand, h0 // 4, 4096)])
                yield
            for h in range(8):
                S.op("dve", lambda e, h=h: e.max(out=tsv.ap[:, h, 0:8], in_=cand.ap[:, h, :]), [sk(cand, h, 1024)], [sk(tsv, h, 64, 0, 2)])
                if h % 4 == 3:
                    yield
            for h in range(8):
                S.op("dve", lambda e, h=h: e.max_index(out=posu.ap[:, h, 0:8], in_max=tsv.ap[:, h, 0:8], in_values=cand.ap[:, h, :]),
                     [sk(cand, h, 1024), sk(tsv, h, 64, 0, 2)], [sk(posu, h, 64, 0, 2)])
                if h % 4 == 3:
                    yield
            for h in range(8):
                S.op("dve", lambda e, h=h: e.match_replace(out=cand.ap[:, h, :], in_to_replace=tsv.ap[:, h, 0:8], in_values=cand.ap[:, h, :], imm_value=NEG),
                     [sk(cand, h, 1024), sk(tsv, h, 64, 0, 2)], [sk(cand, h, 1024)])
                if h % 4 == 3:
                    yield
            for h in range(8):
                S.op("dve", lambda e, h=h: e.max(out=tsv.ap[:, h, 8:16], in_=cand.ap[:, h, :]), [sk(cand, h, 1024)], [sk(tsv, h, 64, 1, 2)])
                if h % 4 == 3:
                    yield
            for h in range(8):
                S.op("dve", lambda e, h=h: e.max_index(out=posu.ap[:, h, 8:16], in_max=tsv.ap[:, h, 8:16], in_values=cand.ap[:, h, :]),
                     [sk(cand, h, 1024), sk(tsv, h, 64, 1, 2)], [sk(posu, h, 64, 1, 2)])
                if h % 4 == 3:
                    yield
            tt("dve", gtf.ap, tsv.ap, tsv.ap[:, :, 0:1].to_broadcast([128, 8, 16]), ALU.subtract, [tsv], [gtf])
            S.op("dve", lambda e: e.tensor_single_scalar(out=posf.ap.bitcast(U32), in_=posu.ap, scalar=15, op=ALU.bitwise_and), K(posu), K(posf))
            yield
            act(gtf.ap, gtf.ap, AF.Exp, [gtf], [gtf])
            cp("dve", pjf.ap, posf.ap.bitcast(U32), [posf], [pjf])
            yield
            S.op("dve", lambda e: e.tensor_single_scalar(out=posf.ap.bitcast(U32), in_=posu.ap, scalar=4, op=ALU.logical_shift_right), K(posu, pjf), K(posf))
            yield
            cp("dve", pif.ap, posf.ap.bitcast(U32), [posf], [pif])
            yield
            S.op("dve", lambda e: e.tensor_reduce(out=zs.ap[:, 0:8], in_=gtf.ap, axis=AX.X, op=ALU.add), K(gtf), K(zs))
            yield
            S.op("dve", lambda e: e.reciprocal(out=zs.ap[:, 0:8], in_=zs.ap[:, 0:8]), K(zs), K(zs))
            yield
            tt("dve", gtf.ap, gtf.ap, zs.ap[:, 0:8].unsqueeze(2).to_broadcast([128, 8, 16]), ALU.mult, [gtf, zs], [gtf])
            yield
            tr(psb[7].ap[:, 0:128], gtf.ap.rearrange("p h k -> p (h k)"), ident_f, [gtf, cst], [psb[7]])
            yield
            cp("act", gT.ap[:, c * 128:(c + 1) * 128], psb[7].ap[:, 0:128], [psb[7]], [gTk[c]])
            yield
            for (pp, hf, dst, dT, keys, bank) in ((pif, 0, eaf, eaT, eaTk, 1), (pjf, 1, ebf, ebT, ebTk, 2)):
                for h0 in (0, 4):
                    ohk = sk(oh, h0 // 4, 4096)
                    tt("dve", oh.ap[:, h0:h0 + 4], iota16.unsqueeze(1).unsqueeze(1).to_broadcast([128, 4, 16, 16]),
                       pp.ap[:, h0:h0 + 4].unsqueeze(3).to_broadcast([128, 4, 16, 16]), ALU.is_equal, [cst, pp], [ohk])
                    yield
                    tt("dve", oh.ap[:, h0:h0 + 4], oh.ap[:, h0:h0 + 4],
                       i12.ap[:, hf, h0:h0 + 4].unsqueeze(2).to_broadcast([128, 4, 16, 16]), ALU.mult, [ohk, i12], [ohk])
                    yield
                    S.op("dve", lambda e, dst=dst, h0=h0: e.tensor_reduce(out=dst.ap[:, h0 * 16:(h0 + 4) * 16],
                                                                     in_=oh.ap[:, h0:h0 + 4].rearrange("p h k i -> p (h k) i"), axis=AX.X, op=ALU.add),
                         [ohk], K(dst))
                    yield
                tr(psb[7].ap[:, bank * 128:(bank + 1) * 128], dst.ap, ident_f, [dst, cst], [psb[7]])
                yield
                cp("act", dT.ap[:, c * 128:(c + 1) * 128], psb[7].ap[:, bank * 128:(bank + 1) * 128], [psb[7]], [keys[c]])
                yield

        def topk_both():
            yield from topk_rest(0)
            yield from topk_rest(1)

        def ggen(b, j):
            NSB = TT // 16
            for sbk in range(NSB):
                i = sbk % 2
                tsl = slice(sbk * 16, (sbk + 1) * 16)
                tt("dve", OAb[i].ap, iota3.ap, eaT.ap[:, tsl].unsqueeze(1).to_broadcast([128, 128, 16]), ALU.is_equal,
                   [iota3, eaTk[sbk // 8]], [OAb[i]])
                tt("dve", OBb[i].ap, iota3.ap, ebT.ap[:, tsl].unsqueeze(1).to_broadcast([128, 128, 16]), ALU.is_equal,
                   [iota3, ebTk[sbk // 8]], [OBb[i]])
                tt("dve", OBb[i].ap, OBb[i].ap, gT.ap[:, tsl].unsqueeze(1).to_broadcast([128, 128, 16]), ALU.mult,
                   [OBb[i], gTk[sbk // 8]], [OBb[i]])
                for q4 in range(4):
                    bank = 4 + (sbk * 4 + q4) % 4
                    for k4 in range(4):
                        tl = q4 * 4 + k4
                        mm(psb[bank].ap[:, k4 * 128:(k4 + 1) * 128], OBb[i].ap[:, :, tl], OAb[i].ap[:, :, tl], True, True,
                           [OBb[i], OAb[i]], [psb[bank]])
                    tg = sbk * 16 + q4 * 4
                    cp("act", Gb.ap[:, tg:tg + 4, :], psb[bank].ap.rearrange("p (t a) -> p t a", t=4), [psb[bank]], [Gb])

        def dense_gen(b, j):
            t0 = j * TT
            hx, hxk, h1T, h1Tk = hxs[j % 2], hxk2[j % 2], h1Ts[j % 2], h1Tk2[j % 2]
            LA = 2
            LD = NUB - 3
            PSH = lambda a: Buf(psb[4 + a % 3].ap[:, 0:TT], psb[4 + a % 3].key)

            def ld(a):
                i = a % NUB
                dma("sp", ubuf[i].ap, u_s[a].rearrange("p (c e) -> p c e", c=8), SCR, [ubuf[i]], "d_ub%d" % i)
                dma("sp", vbuf[i].ap, v_s[a * 128:(a + 1) * 128, :], SCR, [vbuf[i]], "d_vb%d" % i)

            def hid(a):
                i = a % NUB
                if a + LD < 128:
                    ld(a + LD)
                hp = PSH(a)
                for dc in range(8):
                    mm(hp.ap, ubuf[i].ap[:, dc, :], h1T.ap[:, dc, :], dc == 0, dc == 7, [ubuf[i]] + h1Tk, [hp])

            for a in range(LD):
                ld(a)
            for a in range(min(LA, 128)):
                hid(a)
            for a in range(128):
                if a + LA < 128:
                    hid(a + LA)
                i = a % NUB
                k2 = a % 4
                hp = PSH(a)
                act(gel[k2].ap, hp.ap, AF.Gelu_apprx_tanh, [hp], [gel[k2]])
                tt("dve", actb[k2].ap, gel[k2].ap, Gb.ap[:, :, a], ALU.mult, [gel[k2], Gb], [actb[k2]])
                for c in range(2):
                    for hf in range(2):
                        bank = c * 2 + hf
                        mm(psb[bank].ap, actb[k2].ap[:, c * 128:(c + 1) * 128], vbuf[i].ap[:, hf * 512:(hf + 1) * 512],
                           a == 0, a == 127, [actb[k2], vbuf[i]], [psb[bank]])
                yield
            for c in range(2):
                for hf in range(2):
                    bank = c * 2 + hf
                    stt("dve", hx.ap[:, c, hf * 512:(hf + 1) * 512], hx.ap[:, c, hf * 512:(hf + 1) * 512], ALPHA,
                        psb[bank].ap, ALU.mult, ALU.add, [hxk[c], psb[bank]], [hxk[c]])
            run_il(ln_rows(hx, hxk, 2, 0), ln_rows(hx, hxk, 2, 1))
            dma("pool", out_d[b, t0:t0 + TT, :].rearrange("(j p) d -> p j d", p=128), hx.ap, hxk, ["outd"], "d_out")

        def out_h1(b, j):
            t0 = j * TT
            dma("pool", out_d[b, t0:t0 + TT, :].rearrange("(j p) d -> p j d", p=128), hxs[j % 2].ap, hxk2[j % 2], ["outd"], "d_out")

        for b in range(NB):
            pass1(b)
        for b in range(NB):
            S.op("dve", lambda e: e.memset(S_f.ap, 0.0), [], K(S_f))
            if dbg == 1:
                for j in range(NTI):
                    mixer_tile(b, j)
                    out_h1(b, j)
                continue
            mixer_tile(b, 0)
            peer_front(b, 0)
            if dbg == 2:
                continue
            run_il(topk_both())
            if dbg == 3:
                continue
            for j in range(NTI):
                if j + 1 < NTI:
                    mixer_tile(b, j + 1)
                    peer_front(b, j + 1)
                ggen(b, j)
                if dbg == 4:
                    continue
                if j + 2 < NTI:
                    mixer_prefetch(b, j + 2)
                gens = [dense_gen(b, j)]
                if j + 1 < NTI:
                    gens.append(topk_both())
                run_il(*gens)
        S.finish("sp")
        print("ops", S.nops, "waits", S.nwaits, "sbuf_free", nc.sbuf_bytes_remaining)
    return nc


_CACHE = {}


def _consts():
    c = np.zeros((128, 5, 128), np.float32)
    c[:, 0] = np.eye(128)
    c[:, 1] = np.triu(np.ones((128, 128)))
    c[:, 2] = np.tril(np.ones((128, 128)))
    c[:, 3] = np.arange(128)[None, :]
    c[:, 4, 0:16] = np.arange(16)[None, :]
    c[:, 4, 16] = 1.0
    return c


def _shared(inp):
    f = lambda a: np.ascontiguousarray(np.asarray(a, dtype=np.float32))
    rep = lambda a: f(np.broadcast_to(np.asarray(a, np.float32)[None], (128,) + tuple(np.shape(a))))
    u = np.asarray(inp["peer_u"], np.float32)[0].reshape(128, 128, 8, 128)
    return {
        "w_in": f(inp["w_in"][0]), "w_out": f(inp["w_out"][0]), "wq": f(inp["peer_wq"][0]),
        "k1T": f(np.asarray(inp["peer_k1"])[0].transpose(2, 0, 1)),
        "k2T": f(np.asarray(inp["peer_k2"])[0].transpose(2, 0, 1)),
        "uT": f(u.transpose(0, 3, 2, 1).reshape(128, 128, 1024)),
        "vtab": f(inp["peer_v"][0]),
        "lbl": rep(np.asarray(inp["hgrn_lb_logits"])),
        "hgn": rep(np.asarray(inp["hgrn_norm_g"])[0]),
        "glg": rep(np.asarray(inp["gmlp_ln_g"])[0].reshape(512)),
        "glb": rep(np.asarray(inp["gmlp_ln_b"])[0].reshape(512)),
        "wsT": f(np.asarray(inp["gmlp_ws"])[0].transpose(2, 0, 1)),
        "bsT": f(np.asarray(inp["gmlp_bs"])[0].T),
        "lnp": rep(np.stack([np.asarray(inp[k])[0] for k in ("ln1_g", "ln1_b", "ln2_g", "ln2_b")])),
        "cst": _consts(),
    }


def run(inp, ncores, dbg=0):
    x = np.asarray(inp["x"], np.float32)
    B, SEQ, _ = x.shape
    NB = B // ncores
    key = (NB, SEQ, dbg)
    if key not in _CACHE:
        _CACHE[key] = build(NB, SEQ, dbg)
    nc = _CACHE[key]
    sh = _shared(inp)
    maps = []
    for c in range(ncores):
        xs = np.ascontiguousarray(x[c * NB:(c + 1) * NB])
        m = dict(sh)
        m["x"] = xs
        m["xT"] = np.ascontiguousarray(xs.transpose(0, 2, 1))
        maps.append(m)
    res = run_bass_kernel_spmd(nc, maps, core_ids=list(range(ncores)))
    return np.concatenate([r["out"] for r in res.results], axis=0)


def kernel(**inputs):
    return run(inputs, 8).astype(np.float32)
```

```python
import numpy as np
from contextlib import ExitStack
import concourse.bass as bass
import concourse.mybir as mybir
from concourse.bass_utils import run_bass_kernel_spmd

F32 = mybir.dt.float32
BF16 = mybir.dt.bfloat16
U32 = mybir.dt.uint32
AF = mybir.ActivationFunctionType
ALU = mybir.AluOpType
AX = mybir.AxisListType

D = 1024
PT = 3584
NEXP = 16384
ALPHA = 2.0 ** 0.25
LN_EPS = 1e-5
RMS_EPS = 1e-6
TT = 256
NEG = -1e30


class Sync:
    def __init__(self, nc, es):
        self.nc, self.es = nc, es
        self.engs = {"pe": nc.tensor, "act": nc.scalar, "dve": nc.vector,
                     "pool": nc.gpsimd, "sp": nc.sync}
        self.sem, self.cnt, self.inc = {}, {}, {}
        self.known = {e: {} for e in self.engs}
        self.snap = {}
        self.lastw, self.readers = {}, {}
        self.arena = []
        self.nwaits = self.nops = 0
        for e in ("pe", "act", "dve", "pool"):
            self._mk(e, 1)

    def _mk(self, s, inc):
        self.sem[s] = self.es.enter_context(self.nc.semaphore("s_" + s))
        self.cnt[s] = 0
        self.inc[s] = inc


    def _split(self, lo, hi):
        out = []
        for rec in self.arena:
            if rec[1] <= lo or hi <= rec[0]:
                out.append(rec)
                continue
            cuts = [rec[0]] + [c for c in (lo, hi) if rec[0] < c < rec[1]] + [rec[1]]
            for a, b_ in zip(cuts[:-1], cuts[1:]):
                out.append([a, b_, rec[2], dict(rec[3])])
        self.arena = out
        return [rec for rec in self.arena if lo <= rec[0] and rec[1] <= hi]

    def op(self, eng, fn, reads=(), writes=(), dma=None):
        deps = {}

        def add(sq):
            if sq is not None and deps.get(sq[0], 0) < sq[1]:
                deps[sq[0]] = sq[1]

        for r in reads:
            if isinstance(r, tuple):
                for rec in self._split(r[1], r[2]):
                    add(rec[2])
            else:
                add(self.lastw.get(r))
        for w in writes:
            if isinstance(w, tuple):
                for rec in self._split(w[1], w[2]):
                    add(rec[2])
                    for s, q in rec[3].items():
                        add((s, q))
            else:
                add(self.lastw.get(w))
                for s, q in self.readers.get(w, {}).items():
                    add((s, q))
        if dma is not None:
            if dma not in self.sem:
                self._mk(dma, 16)
            if self.cnt[dma] > 0:
                add((dma, self.cnt[dma]))
        stream = dma if dma is not None else eng
        kn = self.known[eng]
        e = self.engs[eng]
        for s, q in deps.items():
            if s == "pe" and stream == "pe":
                continue
            if kn.get(s, 0) >= q:
                continue
            e.wait_ge(self.sem[s], q * self.inc[s])
            self.nwaits += 1
            for s2, q2 in self.snap[(s, q)].items():
                if kn.get(s2, 0) < q2:
                    kn[s2] = q2
            kn[s] = q
        ins = fn(e)
        self.nops += 1
        self.cnt[stream] += 1
        seq = self.cnt[stream]
        ins.then_inc(self.sem[stream], self.inc[stream])
        self.snap[(stream, seq)] = dict(kn)
        for r in reads:
            if isinstance(r, tuple):
                ins_ = sorted(self._split(r[1], r[2]), key=lambda x: x[0])
                pos = r[1]
                for rec in ins_:
                    if rec[0] > pos:
                        self.arena.append([pos, rec[0], None, {stream: seq}])
                    rec[3][stream] = seq
                    pos = rec[1]
                if pos < r[2]:
                    self.arena.append([pos, r[2], None, {stream: seq}])
            else:
                self.readers.setdefault(r, {})[stream] = seq
        for w in writes:
            if isinstance(w, tuple):
                self._split(w[1], w[2])
                self.arena = [rec for rec in self.arena
                              if not (w[1] <= rec[0] and rec[1] <= w[2])]
                self.arena.append([w[1], w[2], (stream, seq), {}])
            else:
                self.lastw[w] = (stream, seq)
                self.readers[w] = {}
        return ins

    def finish(self, eng="sp"):
        e = self.engs[eng]
        for s, c in self.cnt.items():
            if c > 0 and self.known[eng].get(s, 0) < c:
                e.wait_ge(self.sem[s], c * self.inc[s])


class Buf:
    def __init__(self, ap, key):
        self.ap, self.key = ap, key


def build(NB, SEQ, dbg=0):
    NTI = SEQ // TT
    nc = bass.Bass("TRN2", target_bir_lowering=False)
    di = lambda n, s, dt=F32: nc.dram_tensor(n, list(s), dt, kind="ExternalInput").ap()
    x_d = di("x", [NB, SEQ, D])
    xT_d = di("xT", [NB, D, SEQ])
    win_d = di("w_in", [D, PT])
    wout_d = di("w_out", [D, D])
    wq_d = di("wq", [D, 2048])
    k1T_d = di("k1T", [128, 8, 128])
    k2T_d = di("k2T", [128, 8, 128])
    uT_d = di("uT", [128, 128, 1024])
    v_d = di("vtab", [NEXP, D])
    lbl_d = di("lbl", [128, 2, 2, 512])
    hgn_d = di("hgn", [128, 512])
    glg_d = di("glg", [128, 512])
    glb_d = di("glb", [128, 512])
    wsT_d = di("wsT", [128, 4, 128])
    bsT_d = di("bsT", [128, 4])
    lnp_d = di("lnp", [128, 4, D])
    cst_d = di("cst", [128, 5, 128])
    out_d = nc.dram_tensor("out", [NB, SEQ, D], F32, kind="ExternalOutput").ap()
    dscr = lambda n, s, dt=BF16: nc.dram_tensor(n, list(s), dt, kind="Internal").ap()
    win_s = dscr("win_s", [D, PT])
    wout_s = dscr("wout_s", [D, D])
    wq_s = dscr("wq_s", [D, 2048])
    u_s = dscr("u_s", [128, 128, 1024])
    v_s = dscr("v_s", [NEXP, D])
    ck_s = dscr("ck_s", [NB * max(NTI, 1), 128, 512], F32)

    with ExitStack() as es:
        S = Sync(nc, es)
        cnt = [0]

        def sb(shape, dt, name=None):
            cnt[0] += 1
            n = "sb_" + (name or "t%d" % cnt[0])
            t = es.enter_context(nc.sbuf_tensor(n, list(shape), dt))
            return Buf(t[:], n)

        AW = 98496
        arena_t = es.enter_context(nc.sbuf_tensor("arena", [128, AW // 4], F32))

        def av(lo, nbytes, dt, pat=None, **kw):
            ap = arena_t[:, lo // 4:(lo + nbytes) // 4]
            if dt != F32:
                ap = ap.bitcast(dt)
            if pat:
                ap = ap.rearrange(pat, **kw)
            return Buf(ap, ("A", lo, lo + nbytes))

        psb = []
        for i in range(8):
            t = es.enter_context(nc.psum_tensor("ps%d" % i, [128, 512], F32))
            psb.append(Buf(t[:], "ps%d" % i))

        def psbf(i):
            return psb[i].ap.bitcast(BF16)

        def K(*bufs):
            out = []
            for b_ in bufs:
                k_ = b_.key if isinstance(b_, Buf) else b_
                if isinstance(k_, list):
                    out.extend(k_)
                else:
                    out.append(k_)
            return out


        def mm(out, lhsT, rhs, st, sp_, R, W):
            S.op("pe", lambda e: e.matmul(out, lhsT=lhsT, rhs=rhs, start=st, stop=sp_), K(*R), K(*W))

        def tr(out, in_, ident, R, W):
            S.op("pe", lambda e: e.transpose(out, in_, ident), K(*R), K(*W))

        def act(out, in_, func, R, W, **kw):
            S.op("act", lambda e: e.activation(out=out, in_=in_, func=func, **kw), K(*R), K(*W))

        def tt(eng, out, in0, in1, op, R, W):
            S.op(eng, lambda e: e.tensor_tensor(out=out, in0=in0, in1=in1, op=op), K(*R), K(*W))

        def ts(eng, out, in0, s1, s2, op0, op1, R, W):
            if s2 is None:
                S.op(eng, lambda e: e.tensor_scalar(out=out, in0=in0, scalar1=s1, scalar2=None, op0=op0), K(*R), K(*W))
            else:
                S.op(eng, lambda e: e.tensor_scalar(out=out, in0=in0, scalar1=s1, scalar2=s2, op0=op0, op1=op1), K(*R), K(*W))

        def stt(eng, out, in0, sc, in1, op0, op1, R, W):
            S.op(eng, lambda e: e.scalar_tensor_tensor(out=out, in0=in0, scalar=sc, in1=in1, op0=op0, op1=op1), K(*R), K(*W))

        def cp(eng, out, in_, R, W):
            if eng == "act":
                S.op("act", lambda e: e.copy(out=out, in_=in_), K(*R), K(*W))
            else:
                S.op(eng, lambda e: e.tensor_copy(out=out, in_=in_), K(*R), K(*W))

        def rsq(out, in_, scale, eps, R, W):
            act(out, in_, AF.Sqrt, R, W, scale=scale, bias=eps)
            S.op("dve", lambda e: e.reciprocal(out=out, in_=out), K(*W), K(*W))

        def dma(q, out, in_, R, W, stream):
            S.op(q, lambda e: e.dma_start(out=out, in_=in_), K(*R), K(*W), dma=stream)

        cst = sb([128, 5, 128], F32, "cst")
        dma("sp", cst.ap, cst_d, [], [cst], "d_cst")
        ident_f = cst.ap[:, 0, :]
        triU = cst.ap[:, 1, :]
        triL = cst.ap[:, 2, :]
        iota16 = cst.ap[:, 4, 0:16]
        ones_c = cst.ap[:, 4, 16:17]
        cstb = sb([128, 5, 128], BF16, "cstb")
        dma("pool", cstb.ap, cst_d, [], [cstb], "d_cstb")
        ident_b = cstb.ap[:, 0, :]
        iota_b = cstb.ap[:, 3, :]
        iota3 = sb([128, 128, 16], BF16, "iota3")
        cp("dve", iota3.ap, iota_b.unsqueeze(2).to_broadcast([128, 128, 16]), [cstb], [iota3])
        lbl = av(32768, 8192, F32, "p (a b c) -> p a b c", a=2, b=2)
        dma("sp", lbl.ap, lbl_d, [], [lbl], "d_lbl")
        lb = sb([128, 2, 512], F32, "lb")
        oml = sb([128, 2, 512], F32, "oml")
        tt("dve", lb.ap, lbl.ap[:, :, 0, :], lbl.ap[:, :, 1, :], ALU.subtract, [lbl], [lb])
        act(lb.ap, lb.ap, AF.Sigmoid, [lb], [lb])
        ts("dve", oml.ap, lb.ap, -1.0, 1.0, ALU.mult, ALU.add, [lb], [oml])
        hgn = sb([128, 512], F32, "hgn")
        glg = sb([128, 512], F32, "glg")
        glb = sb([128, 512], F32, "glb")
        bsT = sb([128, 4], F32, "bsT")
        lnp = sb([128, 4, D], F32, "lnp")
        dma("sp", hgn.ap, hgn_d, [], [hgn], "d_c1")
        dma("sp", glg.ap, glg_d, [], [glg], "d_c2")
        dma("sp", glb.ap, glb_d, [], [glb], "d_c3")
        dma("sp", bsT.ap, bsT_d, [], [bsT], "d_c4")
        dma("sp", lnp.ap, lnp_d, [], [lnp], "d_c5")
        wsT = sb([128, 4, 128], BF16, "wsT")
        k1T = sb([128, 8, 128], BF16, "k1T")
        k2T = sb([128, 8, 128], BF16, "k2T")
        dma("pool", wsT.ap, wsT_d, [], [wsT], "d_c6")
        dma("pool", k1T.ap, k1T_d, [], [k1T], "d_c7")
        dma("pool", k2T.ap, k2T_d, [], [k2T], "d_c8")

        SCR = ["scr0", "scr1", "scr2", "scr3", "scr4", "scr5", "scr6", "scr7"]
        stg = [av(i * 4096, 4096, BF16) for i in range(4)]
        nst = [0]

        def cast_copy(src, dst):
            i = nst[0] % 4
            nst[0] += 1
            F_ = src.shape[1]
            dma("pool", stg[i].ap[:, 0:F_], src, [], [stg[i]], "d_stg%d" % i)
            dma("sp", dst, stg[i].ap[:, 0:F_], [stg[i]], ["scr%d" % i], "d_sto%d" % i)

        for c in range(8):
            for (c0, c1) in ((0, 2048), (2048, PT)):
                cast_copy(win_d[c * 128:(c + 1) * 128, c0:c1], win_s[c * 128:(c + 1) * 128, c0:c1])
            cast_copy(wout_d[c * 128:(c + 1) * 128, :], wout_s[c * 128:(c + 1) * 128, :])
            cast_copy(wq_d[c * 128:(c + 1) * 128, :], wq_s[c * 128:(c + 1) * 128, :])
        stg2 = [av(o_, 4096, BF16, "p (a f) -> p a f", a=2) for o_ in (65728, 69824, 90304, 94400)]
        nst2 = [0]

        def cast_copy2(src, dst):
            i = nst2[0] % 4
            nst2[0] += 1
            dma("pool", stg2[i].ap, src, [], [stg2[i]], "d_stg2%d" % i)
            dma("sp", dst, stg2[i].ap, [stg2[i]], ["scr%d" % (4 + i)], "d_sto2%d" % i)

        tab_next = [0]

        def copy_tables(n):
            n = 2 * ((n + 1) // 2)
            for a in range(tab_next[0], min(128, tab_next[0] + n), 2):
                cast_copy2(uT_d[a:a + 2].rearrange("a p f -> p a f"), u_s[a:a + 2].rearrange("a p f -> p a f"))
                cast_copy2(v_d[a * 128:(a + 2) * 128, :].rearrange("(a b) f -> b a f", b=128),
                           v_s[a * 128:(a + 2) * 128, :].rearrange("(a b) f -> b a f", b=128))
            tab_next[0] = min(128, tab_next[0] + n)


        S_f = sb([128, 512], F32, "S_f")
        S_b = sb([128, 512], F32, "S_b")
        Sbf = {k: sb([128, 512], BF16, "Sbf_%s" % k) for k in ("f0", "f1", "b0", "b1")}
        hxs = [sb([128, 2, D], F32, "hx%d" % i) for i in range(2)]
        xT = [sb([128, 8, TT], BF16, "xT%d" % i) for i in range(2)]
        h1b = [av(69824 + i * 2048, 2048, BF16) for i in range(2)]
        hxk2 = [["hx%d_c0" % i, "hx%d_c1" % i] for i in range(2)]
        h1Tk2 = [["h1T%d_c0" % i, "h1T%d_c1" % i] for i in range(2)]
        lnst = [dict(st=sb([128, 4, 6], F32, "lst%d" % i), mv=sb([128, 4, 2], F32, "lmv%d" % i), rs=sb([128, 4], F32, "lrs%d" % i), nm=sb([128, 4], F32, "lnm%d" % i)) for i in range(2)]
        h1Ts = [sb([128, 8, TT], BF16, "h1T%d" % i) for i in range(2)]
        yT = [av(65728 + i * 2048, 2048, BF16, "p (c t) -> p c t", c=8) for i in range(2)]
        eaT = sb([128, TT], BF16, "eaT")
        ebT = sb([128, TT], BF16, "ebT")
        gT = sb([128, TT], BF16, "gT")
        NUB = 5
        ubuf = [sb([128, 8, 128], BF16, "ub%d" % i) for i in range(NUB)]
        vbuf = [sb([128, D], BF16, "vb%d" % i) for i in range(NUB)]
        st8 = sb([128, 2, 6], F32, "st8")
        mv = sb([128, 2], F32, "mv")
        rstd = sb([128, 1], F32, "rstd")
        nmr = sb([128, 1], F32, "nmr")

        wsl = [av(73920 + i * 8192, 8192, BF16, "p (c e) -> p c e", c=8) for i in range(3)]
        nws = [0]

        def load_w(src_s, c0, slot=None):
            if slot is None:
                i = nws[0] % 3
                nws[0] += 1
            else:
                i = slot
            dma("sp", wsl[i].ap, src_s[:, c0:c0 + 512].rearrange("(c p) e -> p c e", p=128),
                SCR, [wsl[i]], "d_w%d" % i)
            return wsl[i]

        MB = 0
        off = [MB]

        def mt(nbytes, dt, pat=None, **kw):
            b = av(off[0], nbytes, dt, pat, **kw)
            off[0] += nbytes
            return b

        trn = {}
        for c in range(2):
            for d_ in range(2):
                trn[(c, d_)] = dict(sf=mt(2048, F32), logf=mt(2048, F32), E=mt(2048, F32), qt=mt(1024, BF16))
        pcd = {}
        for c in range(2):
            for d_ in range(2):
                pcd[(c, d_)] = dict(qtT=mt(1024, BF16, "p (h t) -> p h t", h=4),
                                    ktT=mt(1024, BF16, "p (h t) -> p h t", h=4),
                                    ktk=mt(1024, BF16), scm=mt(1024, BF16, "p (h t) -> p h t", h=4),
                                    dec=mt(32, F32))
        pc = [dict(q=mt(2048, F32), vtok=mt(1024, BF16), sg=mt(2048, F32), gu=mt(2048, F32),
                   vc=mt(1024, BF16), y=mt(2048, BF16), ss=mt(32, F32),
                   gv=trn[(c, 0)]["sf"], osb=trn[(c, 0)]["logf"], sq=trn[(c, 0)]["E"]) for c in range(2)]
        assert off[0] <= 65728, off[0]
        qT = av(0, 8192, BF16, "p (c t) -> p c t", c=16)
        oh = av(24576, 8192, F32, "p (h k i) -> p h k i", h=8, k=16)
        TK = []
        for c_, (b0, sm) in enumerate(((8192, 32768), (40960, 57344))):
            TK.append(dict(
                s12=av(b0, 8192, F32, "p (a h k) -> p a h k", a=2, h=8),
                cand=av(b0 + 8192, 8192, F32, "p (h k) -> p h k", h=8),
                v12=av(sm, 1024, F32, "p (a h k) -> p a h k", a=2, h=8),
                i12u=av(sm + 1024, 1024, U32, "p (a h k) -> p a h k", a=2, h=8),
                i12=av(sm + 2048, 1024, F32, "p (a h k) -> p a h k", a=2, h=8),
                tsv=av(sm + 3072, 512, F32, "p (h k) -> p h k", h=8),
                posu=av(sm + 3584, 512, U32, "p (h k) -> p h k", h=8),
                posf=av(sm + 4096, 512, F32, "p (h k) -> p h k", h=8),
                pjf=av(sm + 4608, 512, F32, "p (h k) -> p h k", h=8),
                pif=av(sm + 5120, 512, F32, "p (h k) -> p h k", h=8),
                eaf=av(sm + 5632, 512, F32),
                ebf=av(sm + 6144, 512, F32),
                gtf=av(sm + 6656, 512, F32, "p (h k) -> p h k", h=8),
                zs=av(sm + 7168, 32, F32)))
        TK[0]["cand"] = av(16384, 8192, F32, "p (h k) -> p h k", h=8)
        TK[1]["s12"] = av(40960, 8192, F32, "p (a h k) -> p a h k", a=2, h=8)
        TK[1]["cand"] = av(49152, 8192, F32, "p (h k) -> p h k", h=8)
        Gb = av(0, 65536, BF16, "p (t a) -> p t a", a=128)
        OAb = [av(81920 + i * 2048, 2048, BF16, "p (a t) -> p a t", t=8) for i in range(4)]
        OBb = [av(90112 + i * 2048, 2048, BF16, "p (a t) -> p a t", t=8) for i in range(4)]
        gel = [sb([128, TT], F32, "gel%d" % i) for i in range(4)]
        actb = [sb([128, TT], BF16, "actb%d" % i) for i in range(4)]

        def ln_rows(z, zkeys, gi, c):
            zz = z.ap[:, c, :]
            zk = zkeys[c]
            st = lnst[c]
            for hf in range(2):
                S.op("dve", lambda e, hf=hf: e.bn_stats(out=st["st"].ap[:, hf, :], in_=zz[:, hf * 512:(hf + 1) * 512]),
                     K(zk), K(st["st"]))
            yield
            S.op("dve", lambda e: e.bn_aggr(out=st["mv"].ap[:, 0, :], in_=st["st"].ap[:, 0:2, :].rearrange("p a b -> p (a b)")), K(st["st"]), K(st["mv"]))
            yield
            act(st["rs"].ap[:, 0:1], st["mv"].ap[:, 0, 1:2], AF.Sqrt, [st["mv"]], [st["rs"]], scale=1.0, bias=LN_EPS)
            yield
            S.op("dve", lambda e: e.reciprocal(out=st["rs"].ap[:, 0:1], in_=st["rs"].ap[:, 0:1]), K(st["rs"]), K(st["rs"]))
            yield
            ts("dve", st["nm"].ap[:, 0:1], st["mv"].ap[:, 0, 0:1], -1.0, st["rs"].ap[:, 0:1], ALU.mult, ALU.mult, [st["mv"], st["rs"]], [st["nm"]])
            yield
            act(zz, zz, AF.Identity, [zk, st["nm"], st["rs"]], [zk], scale=st["rs"].ap[:, 0:1], bias=st["nm"].ap[:, 0:1])
            yield
            tt("dve", zz, zz, lnp.ap[:, gi, :], ALU.mult, [zk, lnp], [zk])
            yield
            tt("dve", zz, zz, lnp.ap[:, gi + 1, :], ALU.add, [zk, lnp], [zk])
            yield

        def proj(xt, c, wslot, bank):
            for dc in range(8):
                mm(psb[bank].ap, xt.ap[:, dc, c * 128:(c + 1) * 128], wslot.ap[:, dc, :], dc == 0, dc == 7,
                   [xt, wslot], [psb[bank]])

        def run_il(*gens):
            gens = list(gens)
            while gens:
                for g_ in list(gens):
                    try:
                        next(g_)
                    except StopIteration:
                        gens.remove(g_)

        def gate_chain(xt, wslot, c, d_, zc, tb, toff, need_q, qsrc, bkey=None):
            bkey = bkey or (c, d_)
            T_ = trn[bkey]
            P_ = pcd[bkey]
            Z = psb[zc]
            proj(xt, c, wslot, zc)
            yield
            act(T_["sf"].ap, Z.ap, AF.Sigmoid, [Z], [T_["sf"]])
            yield
            tt("dve", T_["sf"].ap, T_["sf"].ap, oml.ap[:, d_, :], ALU.mult, [T_["sf"], oml], [T_["sf"]])
            yield
            tt("dve", T_["sf"].ap, T_["sf"].ap, lb.ap[:, d_, :], ALU.add, [T_["sf"], lb], [T_["sf"]])
            yield
            act(T_["logf"].ap, T_["sf"].ap, AF.Ln, [T_["sf"]], [T_["logf"]])
            yield
            ts("pool", T_["sf"].ap, T_["sf"].ap, -1.0, 1.0, ALU.mult, ALU.add, [T_["sf"]], [T_["sf"]])
            tri = triU if d_ == 0 else triL
            mm(Z.ap, tri, T_["logf"].ap, True, True, [cst, T_["logf"]], [Z])
            q4 = (2 * bkey[0] + bkey[1]) * 4
            for h in range(4):
                mm(psb[6].ap[:, q4 + h:q4 + h + 1], T_["logf"].ap[:, h * 128:(h + 1) * 128], ones_c, True, True,
                   [T_["logf"], cst], [psb[6]])
            yield
            act(T_["E"].ap, Z.ap, AF.Exp, [Z], [T_["E"]], scale=-1.0)
            act(P_["dec"].ap[:, 0:4], psb[6].ap[:, q4:q4 + 4], AF.Exp, [psb[6]], [P_["dec"]])
            yield
            tt("dve", P_["ktk"].ap, T_["sf"].ap, T_["E"].ap, ALU.mult, [T_["sf"], T_["E"]], [P_["ktk"]])
            yield
            if need_q:
                act(T_["E"].ap, Z.ap, AF.Exp, [Z, P_["ktk"]], [T_["E"]])
                pb = psbf(tb)
                for h in range(4):
                    tr(pb[:, toff + h * 128:toff + (h + 1) * 128], P_["ktk"].ap[:, h * 128:(h + 1) * 128], ident_b,
                       [P_["ktk"], cstb], [psb[tb]])
                yield
                cp("act", P_["ktT"].ap, pb[:, toff:toff + 512].rearrange("p (h t) -> p h t", h=4), [psb[tb]], [P_["ktT"]])
                tt("dve", T_["qt"].ap, qsrc.ap, T_["E"].ap, ALU.mult, [qsrc, T_["E"]], [T_["qt"]])
                yield
                for h in range(4):
                    tr(pb[:, toff + h * 128:toff + (h + 1) * 128], T_["qt"].ap[:, h * 128:(h + 1) * 128], ident_b,
                       [T_["qt"], cstb], [psb[tb]])
                yield
                cp("act", P_["qtT"].ap, pb[:, toff:toff + 512].rearrange("p (h t) -> p h t", h=4), [psb[tb]], [P_["qtT"]])
                yield

        def state_update(Sst, c, d_, vtok, bank, snap_to=None, bkey=None):
            P_ = pcd[bkey or (c, d_)]
            if snap_to is not None:
                cp("act", snap_to.ap, Sst.ap, [Sst], [snap_to])
            for h in range(4):
                mm(psb[bank].ap[:, h * 128:(h + 1) * 128], P_["ktk"].ap[:, h * 128:(h + 1) * 128],
                   vtok.ap[:, h * 128:(h + 1) * 128], True, True, [P_["ktk"], vtok], [psb[bank]])
            yield
            tt("dve", Sst.ap, Sst.ap, psb[bank].ap, ALU.add, [Sst, psb[bank]], [Sst])
            yield
            for h in range(4):
                act(Sst.ap[:, h * 128:(h + 1) * 128], Sst.ap[:, h * 128:(h + 1) * 128], AF.Copy, [Sst, P_["dec"]], [Sst],
                    scale=P_["dec"].ap[:, h:h + 1])
            yield

        nx = [0]

        def load_xT(b, j):
            i = nx[0] % 2
            nx[0] += 1
            dma("pool", xT[i].ap, xT_d[b, :, j * TT:(j + 1) * TT].rearrange("(c p) t -> p c t", p=128),
                [], [xT[i]], "d_xT%d" % i)
            return xT[i]

        def pass1(b):
            S.op("dve", lambda e: e.memset(S_b.ap, 0.0), [], K(S_b))
            if NTI < 2:
                dma("sp", ck_s[b * NTI], S_b.ap, [S_b], ["ck%d_0" % b], "d_cko")
                copy_tables(128)
                return
            wzb = load_w(win_s, 1024, slot=0)
            wi = load_w(win_s, 1536, slot=1)
            nper = -(-128 // max(NB * (NTI - 1), 1))

            def chains(j, st):
                xt = load_xT(b, j)
                vt = [pc[c]["vtok"] if st == 1 else pc[c]["vc"] for c in range(2)]

                def vchain(c):
                    proj(xt, c, wi, 4 + c)
                    yield
                    cp("act", vt[c].ap, psb[4 + c].ap, [psb[4 + c]], [vt[c]])
                    yield

                zb0 = 0 if st == 1 else 2
                return [gate_chain(xt, wzb, 1, 1, zb0, 4, 0, False, None, bkey=(1, st)),
                        gate_chain(xt, wzb, 0, 1, zb0 + 1, 5, 0, False, None, bkey=(0, st)),
                        vchain(1), vchain(0)]

            def updates(j, st):
                vt = [pc[c]["vtok"] if st == 1 else pc[c]["vc"] for c in range(2)]
                dma("sp", ck_s[b * NTI + j], S_b.ap, [S_b], ["ck%d_%d" % (b, j)], "d_cko")
                for c in (1, 0):
                    yield from state_update(S_b, c, 1, vt[c], 7, bkey=(c, st))

            st = 1
            run_il(*chains(NTI - 1, st))
            for j in range(NTI - 1, 0, -1):
                nxt = chains(j - 1, 1 - st) if j - 1 >= 1 else []
                run_il(updates(j, st), *nxt)
                st = 1 - st
                copy_tables(nper)
            dma("sp", ck_s[b * NTI], S_b.ap, [S_b], ["ck%d_0" % b], "d_cko")
            if b == NB - 1:
                copy_tables(128)

        pre = {}
        prewq = {}

        def mixer_prefetch(b, j):
            xt = load_xT(b, j)
            dma("sp", S_b.ap, ck_s[b * NTI + j], ["ck%d_%d" % (b, j)], [S_b], "d_cki")
            pre[(b, j)] = xt

        def mixer_tile(b, j):
            t0 = j * TT
            hx, hxk, h1T, h1Tk = hxs[j % 2], hxk2[j % 2], h1Ts[j % 2], h1Tk2[j % 2]
            if (b, j) not in pre:
                mixer_prefetch(b, j)
            xt = pre.pop((b, j))
            wq_, wi_, wg_ = load_w(win_s, 0), load_w(win_s, 3 * 512), load_w(win_s, 4 * 512)

            def evac(c, w, bank, kind):
                proj(xt, c, w, bank)
                pb = psb[bank]
                yield
                if kind == "q":
                    cp("act", pc[c]["q"].ap, pb.ap, [pb], [pc[c]["q"]])
                elif kind == "i":
                    cp("act", pc[c]["vtok"].ap, pb.ap, [pb], [pc[c]["vtok"]])
                elif kind == "g":
                    act(pc[c]["sg"].ap, pb.ap, AF.Silu, [pb], [pc[c]["sg"]])
                    yield
                    tt("pool", pc[c]["sg"].ap, pc[c]["sg"].ap, hgn.ap, ALU.mult, [pc[c]["sg"], hgn], [pc[c]["sg"]])
                elif kind == "u":
                    act(pc[c]["gu"].ap, pb.ap, AF.Gelu_apprx_tanh, [pb], [pc[c]["gu"]])
                else:
                    gv = pc[c]["gv"]
                    act(gv.ap, pb.ap, AF.Gelu_apprx_tanh, [pb], [gv])
                    yield
                    st = lnst[c]
                    for gg in range(4):
                        sl = gv.ap[:, gg * 128:(gg + 1) * 128]
                        S.op("dve", lambda e, sl=sl, gg=gg: e.bn_stats(out=st["st"].ap[:, gg, :], in_=sl), K(gv), K(st["st"]))
                    yield
                    for gg in range(4):
                        S.op("dve", lambda e, gg=gg: e.bn_aggr(out=st["mv"].ap[:, gg, :], in_=st["st"].ap[:, gg, :]), K(st["st"]), K(st["mv"]))
                    yield
                    act(st["rs"].ap, st["mv"].ap[:, :, 1], AF.Sqrt, [st["mv"]], [st["rs"]], scale=1.0, bias=LN_EPS)
                    yield
                    S.op("dve", lambda e: e.reciprocal(out=st["rs"].ap, in_=st["rs"].ap), K(st["rs"]), K(st["rs"]))
                    yield
                    stt("dve", st["nm"].ap, st["mv"].ap[:, :, 0], -1.0, st["rs"].ap, ALU.mult, ALU.mult, [st["mv"], st["rs"]], [st["nm"]])
                    yield
                    for gg in range(4):
                        sl = gv.ap[:, gg * 128:(gg + 1) * 128]
                        act(sl, sl, AF.Identity, [gv, st["nm"], st["rs"]], [gv], scale=st["rs"].ap[:, gg:gg + 1], bias=st["nm"].ap[:, gg:gg + 1])
                    yield
                    tt("pool", gv.ap, gv.ap, glg.ap, ALU.mult, [gv, glg], [gv])
                    yield
                    tt("pool", pc[c]["vc"].ap, gv.ap, glb.ap, ALU.add, [gv, glb], [pc[c]["vc"]])
                yield

            run_il(evac(0, wq_, 0, "q"), evac(1, wq_, 1, "q"), evac(0, wi_, 2, "i"), evac(1, wi_, 3, "i"),
                   evac(0, wg_, 4, "g"), evac(1, wg_, 5, "g"))
            wzf = load_w(win_s, 512)
            wzb = load_w(win_s, 1024)
            dma("sp", hx.ap, x_d[b, t0:t0 + TT, :].rearrange("(j p) d -> p j d", p=128), [], hxk, "d_hx")
            run_il(gate_chain(xt, wzf, 0, 0, 0, 4, 0, True, pc[0]["q"]), gate_chain(xt, wzf, 1, 0, 1, 5, 0, True, pc[1]["q"]),
                   gate_chain(xt, wzb, 0, 1, 2, 4, 512, True, pc[0]["q"]), gate_chain(xt, wzb, 1, 1, 3, 5, 512, True, pc[1]["q"]))
            wu_ = load_w(win_s, 5 * 512)
            wv_ = load_w(win_s, 6 * 512)

            def fwd_chain():
                yield from state_update(S_f, 0, 0, pc[0]["vtok"], 4, Sbf["f0"])
                yield from state_update(S_f, 1, 0, pc[1]["vtok"], 4, Sbf["f1"])

            def bwd_chain():
                yield from state_update(S_b, 1, 1, pc[1]["vtok"], 5, Sbf["b1"])
                cp("act", Sbf["b0"].ap, S_b.ap, [S_b], [Sbf["b0"]])
                yield

            run_il(evac(0, wu_, 0, "u"), evac(1, wu_, 1, "u"), evac(0, wv_, 2, "v"), evac(1, wv_, 3, "v"),
                   fwd_chain(), bwd_chain())
            wo = [load_w(wout_s, 0), load_w(wout_s, 512)]
            prewq[(b, j)] = load_w(wq_s, 0)

            def part2(c):
                P = pc[c]
                B0 = 4 * c
                for d_ in range(2):
                    Q = pcd[(c, d_)]
                    bk = psb[B0 + d_]
                    for h in range(4):
                        mm(bk.ap[:, h * 128:(h + 1) * 128], Q["ktT"].ap[:, h, :], Q["qtT"].ap[:, h, :], True, True,
                           [Q["ktT"], Q["qtT"]], [bk])
                    yield
                    tri = triU if d_ == 0 else triL
                    tt("dve", Q["scm"].ap, bk.ap.rearrange("p (h t) -> p h t", h=4),
                       tri.unsqueeze(1).to_broadcast([128, 4, 128]), ALU.mult, [bk, cst], [Q["scm"]])
                    yield
                ob = psb[B0 + 2]
                for h in range(4):
                    o_ = ob.ap[:, h * 128:(h + 1) * 128]
                    vs = P["vtok"].ap[:, h * 128:(h + 1) * 128]
                    Qf, Qb = pcd[(c, 0)], pcd[(c, 1)]
                    sf_, sb_ = Sbf["f%d" % c], Sbf["b%d" % c]
                    mm(o_, Qf["scm"].ap[:, h, :], vs, True, False, [Qf["scm"], P["vtok"]], [ob])
                    mm(o_, Qf["qtT"].ap[:, h, :], sf_.ap[:, h * 128:(h + 1) * 128], False, False, [Qf["qtT"], sf_], [ob])
                    mm(o_, Qb["scm"].ap[:, h, :], vs, False, False, [Qb["scm"], P["vtok"]], [ob])
                    mm(o_, Qb["qtT"].ap[:, h, :], sb_.ap[:, h * 128:(h + 1) * 128], False, True, [Qb["qtT"], sb_], [ob])
                gb_ = psb[B0 + 3]
                for gg in range(4):
                    mm(gb_.ap[:, gg * 128:(gg + 1) * 128], wsT.ap[:, gg, :], P["vc"].ap[:, gg * 128:(gg + 1) * 128],
                       True, True, [wsT, P["vc"]], [gb_])
                yield
                cp("act", P["osb"].ap, ob.ap, [ob], [P["osb"]])
                for gg in range(4):
                    stt("dve", P["y"].ap[:, 512 + gg * 128:512 + (gg + 1) * 128], gb_.ap[:, gg * 128:(gg + 1) * 128],
                        bsT.ap[:, gg:gg + 1], P["gu"].ap[:, gg * 128:(gg + 1) * 128], ALU.add, ALU.mult,
                        [gb_, bsT, P["gu"]], [P["y"]])
                yield
                sq = P["sq"]
                tt("pool", sq.ap, P["osb"].ap, P["osb"].ap, ALU.mult, [P["osb"]], [sq])
                yield
                S.op("dve", lambda e: e.tensor_reduce(out=P["ss"].ap[:, 0:4], in_=sq.ap.rearrange("p (h e) -> p h e", h=4),
                                                      axis=AX.X, op=ALU.add), K(sq), K(P["ss"]))
                yield
                act(P["ss"].ap[:, 0:4], P["ss"].ap[:, 0:4], AF.Sqrt, [P["ss"]], [P["ss"]], scale=1.0 / 128, bias=RMS_EPS)
                tt("pool", P["osb"].ap, P["osb"].ap, P["sg"].ap, ALU.mult, [P["osb"], P["sg"]], [P["osb"]])
                yield
                S.op("dve", lambda e: e.reciprocal(out=P["ss"].ap[:, 0:4], in_=P["ss"].ap[:, 0:4]), K(P["ss"]), K(P["ss"]))
                yield
                for h in range(4):
                    ts("dve", P["y"].ap[:, h * 128:(h + 1) * 128], P["osb"].ap[:, h * 128:(h + 1) * 128],
                       P["ss"].ap[:, h:h + 1], None, ALU.mult, None, [P["osb"], P["ss"]], [P["y"]])
                yield
                tb_ = psb[B0]
                pb = psbf(B0)
                for ec in range(8):
                    tr(pb[:, ec * 128:(ec + 1) * 128], P["y"].ap[:, ec * 128:(ec + 1) * 128], ident_b, [P["y"], cstb], [tb_])
                yield
                cp("act", yT[c].ap, pb[:, 0:1024].rearrange("p (c t) -> p c t", c=8), [tb_], [yT[c]])
                yield
                for hf in range(2):
                    bk = psb[B0 + 1 + hf]
                    for ec in range(8):
                        mm(bk.ap, yT[c].ap[:, ec, :], wo[hf].ap[:, ec, :], ec == 0, ec == 7, [yT[c], wo[hf]], [bk])
                yield
                for hf in range(2):
                    bk = psb[B0 + 1 + hf]
                    stt("dve", hx.ap[:, c, hf * 512:(hf + 1) * 512], hx.ap[:, c, hf * 512:(hf + 1) * 512], ALPHA,
                        bk.ap, ALU.mult, ALU.add, [hxk[c], bk], [hxk[c]])
                yield
                yield from ln_rows(hx, hxk, 0, c)
                cp("act", h1b[c].ap, hx.ap[:, c, :], [hxk[c]], [h1b[c]])
                yield
                pb = psbf(B0 + 3)
                for dc in range(8):
                    tr(pb[:, dc * 128:(dc + 1) * 128], h1b[c].ap[:, dc * 128:(dc + 1) * 128], ident_b, [h1b[c], cstb], [psb[B0 + 3]])
                yield
                cp("act", h1T.ap[:, :, c * 128:(c + 1) * 128], pb[:, 0:1024].rearrange("p (c t) -> p c t", c=8),
                   [psb[B0 + 3]], [h1Tk[c]])
                yield

            run_il(part2(0), part2(1))

        qT = av(0, 8192, BF16, "p (c t) -> p c t", c=16)
        s12c = [av(65536 + c_ * 8192, 8192, F32, "p (a h k) -> p a h k", a=2, h=8) for c_ in range(2)]
        cand = av(81920, 8192, F32, "p (h k) -> p h k", h=8)
        oh = av(81920, 8192, F32, "p (h k i) -> p h k i", h=8, k=16)
        sm = 90112
        v12 = av(sm, 1024, F32, "p (a h k) -> p a h k", a=2, h=8)
        i12u = av(sm + 1024, 1024, U32, "p (a h k) -> p a h k", a=2, h=8)
        i12 = av(sm + 2048, 1024, F32, "p (a h k) -> p a h k", a=2, h=8)
        tsv = av(sm + 3072, 512, F32, "p (h k) -> p h k", h=8)
        posu = av(sm + 3584, 512, U32, "p (h k) -> p h k", h=8)
        posf = av(sm + 4096, 512, F32, "p (h k) -> p h k", h=8)
        pjf = av(sm + 4608, 512, F32, "p (h k) -> p h k", h=8)
        pif = av(sm + 5120, 512, F32, "p (h k) -> p h k", h=8)
        eaf = av(sm + 5632, 512, F32)
        ebf = av(sm + 6144, 512, F32)
        gtf = av(sm + 6656, 512, F32, "p (h k) -> p h k", h=8)
        zs = av(sm + 7168, 32, F32)
        sk = lambda buf, idx, n, part=0, np_=1: ("A", buf.key[1] + idx * n + part * (n // np_), buf.key[1] + idx * n + (part + 1) * (n // np_))
        HH = [(hf, h) for hf in range(2) for h in range(8)]
        eaTk = ["eaT0", "eaT1"]
        ebTk = ["ebT0", "ebT1"]
        gTk = ["gT0", "gT1"]

        def peer_front(b, j):
            h1T, h1Tk = h1Ts[j % 2], h1Tk2[j % 2]
            for g in range(4):
                w = prewq.pop((b, j), None) if g == 0 else None
                if w is None:
                    w = load_w(wq_s, g * 512)
                for cc in range(4):
                    ce = g * 4 + cc
                    bank = ce % 2
                    for dc in range(8):
                        mm(psb[bank].ap[:, 0:TT], w.ap[:, dc, cc * 128:(cc + 1) * 128], h1T.ap[:, dc, :], dc == 0, dc == 7,
                           [w] + h1Tk, [psb[bank]])
                    cp("act", qT.ap[:, ce, :], psb[bank].ap[:, 0:TT], [psb[bank]], [qT])
            for c in range(2):
                PB = 4 * c
                for hf in range(2):
                    kT = k1T if hf == 0 else k2T
                    for h in range(8):
                        bank = PB + hf * 2 + h // 4
                        mm(psb[bank].ap[:, (h % 4) * 128:(h % 4 + 1) * 128], qT.ap[:, 2 * h + hf, c * 128:(c + 1) * 128],
                           kT.ap[:, h, :], True, True, [qT, kT], [psb[bank]])
                for hf in range(2):
                    for q4 in range(2):
                        bank = PB + hf * 2 + q4
                        cp("act", s12c[c].ap[:, hf, q4 * 4:(q4 + 1) * 4, :], psb[bank].ap.rearrange("p (h k) -> p h k", h=4),
                           [psb[bank]], [sk(s12c[c], hf * 2 + q4, 2048)])

        def topk_rest(c):
            s12 = s12c[c]
            for (hf, h) in HH:
                q = hf * 8 + h
                S.op("dve", lambda e, hf=hf, h=h: e.max(out=v12.ap[:, hf, h, 0:8], in_=s12.ap[:, hf, h, :]),
                     [sk(s12, q, 512)], [sk(v12, q, 64, 0, 2)])
                if q % 4 == 3:
                    yield
            for (hf, h) in HH:
                q = hf * 8 + h
                S.op("dve", lambda e, hf=hf, h=h: e.max_index(out=i12u.ap[:, hf, h, 0:8], in_max=v12.ap[:, hf, h, 0:8], in_values=s12.ap[:, hf, h, :]),
                     [sk(s12, q, 512), sk(v12, q, 64, 0, 2)], [sk(i12u, q, 64, 0, 2)])
                if q % 4 == 3:
                    yield
            for (hf, h) in HH:
                q = hf * 8 + h
                S.op("dve", lambda e, hf=hf, h=h: e.match_replace(out=s12.ap[:, hf, h, :], in_to_replace=v12.ap[:, hf, h, 0:8], in_values=s12.ap[:, hf, h, :], imm_value=NEG),
                     [sk(s12, q, 512), sk(v12, q, 64, 0, 2)], [sk(s12, q, 512)])
                if q % 4 == 3:
                    yield
            for (hf, h) in HH:
                q = hf * 8 + h
                S.op("dve", lambda e, hf=hf, h=h: e.max(out=v12.ap[:, hf, h, 8:16], in_=s12.ap[:, hf, h, :]),
                     [sk(s12, q, 512)], [sk(v12, q, 64, 1, 2)])
                if q % 4 == 3:
                    yield
            for (hf, h) in HH:
                q = hf * 8 + h
                S.op("dve", lambda e, hf=hf, h=h: e.max_index(out=i12u.ap[:, hf, h, 8:16], in_max=v12.ap[:, hf, h, 8:16], in_values=s12.ap[:, hf, h, :]),
                     [sk(s12, q, 512), sk(v12, q, 64, 1, 2)], [sk(i12u, q, 64, 1, 2)])
                if q % 4 == 3:
                    yield
            cp("dve", i12.ap, i12u.ap, [i12u], [i12])
            yield
            for h0 in (0, 4):
                tt("dve", cand.ap[:, h0:h0 + 4].rearrange("p h (i j) -> p h i j", i=16),
                   v12.ap[:, 0, h0:h0 + 4].unsqueeze(3).to_broadcast([128, 4, 16, 16]),
                   v12.ap[:, 1, h0:h0 + 4].unsqueeze(2).to_broadcast([128, 4, 16, 16]), ALU.add, [v12], [sk(cand, h0 // 4, 4096)])
                yield
            for h in range(8):
                S.op("dve", lambda e, h=h: e.max(out=tsv.ap[:, h, 0:8], in_=cand.ap[:, h, :]), [sk(cand, h, 1024)], [sk(tsv, h, 64, 0, 2)])
                if h % 4 == 3:
                    yield
            for h in range(8):
                S.op("dve", lambda e, h=h: e.max_index(out=posu.ap[:, h, 0:8], in_max=tsv.ap[:, h, 0:8], in_values=cand.ap[:, h, :]),
                     [sk(cand, h, 1024), sk(tsv, h, 64, 0, 2)], [sk(posu, h, 64, 0, 2)])
                if h % 4 == 3:
                    yield
            for h in range(8):
                S.op("dve", lambda e, h=h: e.match_replace(out=cand.ap[:, h, :], in_to_replace=tsv.ap[:, h, 0:8], in_values=cand.ap[:, h, :], imm_value=NEG),
                     [sk(cand, h, 1024), sk(tsv, h, 64, 0, 2)], [sk(cand, h, 1024)])
                if h % 4 == 3:
                    yield
            for h in range(8):
                S.op("dve", lambda e, h=h: e.max(out=tsv.ap[:, h, 8:16], in_=cand.ap[:, h, :]), [sk(cand, h, 1024)], [sk(tsv, h, 64, 1, 2)])
                if h % 4 == 3:
                    yield
            for h in range(8):
                S.op("dve", lambda e, h=h: e.max_index(out=posu.ap[:, h, 8:16], in_max=tsv.ap[:, h, 8:16], in_values=cand.ap[:, h, :]),
                     [sk(cand, h, 1024), sk(tsv, h, 64, 1, 2)], [sk(posu, h, 64, 1, 2)])
                if h % 4 == 3:
                    yield
            tt("dve", gtf.ap, tsv.ap, tsv.ap[:, :, 0:1].to_broadcast([128, 8, 16]), ALU.subtract, [tsv], [gtf])
            S.op("dve", lambda e: e.tensor_single_scalar(out=posf.ap.bitcast(U32), in_=posu.ap, scalar=15, op=ALU.bitwise_and), K(posu), K(posf))
            yield
            act(gtf.ap, gtf.ap, AF.Exp, [gtf], [gtf])
            cp("dve", pjf.ap, posf.ap.bitcast(U32), [posf], [pjf])
            yield
            S.op("dve", lambda e: e.tensor_single_scalar(out=posf.ap.bitcast(U32), in_=posu.ap, scalar=4, op=ALU.logical_shift_right), K(posu, pjf), K(posf))
            yield
            cp("dve", pif.ap, posf.ap.bitcast(U32), [posf], [pif])
            yield
            S.op("dve", lambda e: e.tensor_reduce(out=zs.ap[:, 0:8], in_=gtf.ap, axis=AX.X, op=ALU.add), K(gtf), K(zs))
            yield
            S.op("dve", lambda e: e.reciprocal(out=zs.ap[:, 0:8], in_=zs.ap[:, 0:8]), K(zs), K(zs))
            yield
            tt("dve", gtf.ap, gtf.ap, zs.ap[:, 0:8].unsqueeze(2).to_broadcast([128, 8, 16]), ALU.mult, [gtf, zs], [gtf])
            yield
            tr(psb[7].ap[:, 0:128], gtf.ap.rearrange("p h k -> p (h k)"), ident_f, [gtf, cst], [psb[7]])
            yield
            cp("act", gT.ap[:, c * 128:(c + 1) * 128], psb[7].ap[:, 0:128], [psb[7]], [gTk[c]])
            yield
            for (pp, hf, dst, dT, keys, bank) in ((pif, 0, eaf, eaT, eaTk, 1), (pjf, 1, ebf, ebT, ebTk, 2)):
                for h0 in (0, 4):
                    ohk = sk(oh, h0 // 4, 4096)
                    tt("dve", oh.ap[:, h0:h0 + 4], iota16.unsqueeze(1).unsqueeze(1).to_broadcast([128, 4, 16, 16]),
                       pp.ap[:, h0:h0 + 4].unsqueeze(3).to_broadcast([128, 4, 16, 16]), ALU.is_equal, [cst, pp], [ohk])
                    yield
                    tt("dve", oh.ap[:, h0:h0 + 4], oh.ap[:, h0:h0 + 4],
                       i12.ap[:, hf, h0:h0 + 4].unsqueeze(2).to_broadcast([128, 4, 16, 16]), ALU.mult, [ohk, i12], [ohk])
                    yield
                    S.op("dve", lambda e, dst=dst, h0=h0: e.tensor_reduce(out=dst.ap[:, h0 * 16:(h0 + 4) * 16],
                                                                     in_=oh.ap[:, h0:h0 + 4].rearrange("p h k i -> p (h k) i"), axis=AX.X, op=ALU.add),
                         [ohk], K(dst))
                    yield
                tr(psb[7].ap[:, bank * 128:(bank + 1) * 128], dst.ap, ident_f, [dst, cst], [psb[7]])
                yield
                cp("act", dT.ap[:, c * 128:(c + 1) * 128], psb[7].ap[:, bank * 128:(bank + 1) * 128], [psb[7]], [keys[c]])
                yield

        def topk_both():
            yield from topk_rest(0)
            yield from topk_rest(1)

        def ggen(b, j):
            NSB = TT // 8
            for sbk in range(NSB):
                i = sbk % 4
                tsl = slice(sbk * 8, (sbk + 1) * 8)
                io3 = iota3.ap[:, :, 0:8]
                tt("dve", OAb[i].ap, io3, eaT.ap[:, tsl].unsqueeze(1).to_broadcast([128, 128, 8]), ALU.is_equal,
                   [iota3, eaTk[sbk // 16]], [OAb[i]])
                tt("dve", OBb[i].ap, io3, ebT.ap[:, tsl].unsqueeze(1).to_broadcast([128, 128, 8]), ALU.is_equal,
                   [iota3, ebTk[sbk // 16]], [OBb[i]])
                tt("dve", OBb[i].ap, OBb[i].ap, gT.ap[:, tsl].unsqueeze(1).to_broadcast([128, 128, 8]), ALU.mult,
                   [OBb[i], gTk[sbk // 16]], [OBb[i]])
                for q4 in range(2):
                    bank = 4 + (sbk * 2 + q4) % 4
                    for k4 in range(4):
                        tl = q4 * 4 + k4
                        mm(psb[bank].ap[:, k4 * 128:(k4 + 1) * 128], OBb[i].ap[:, :, tl], OAb[i].ap[:, :, tl], True, True,
                           [OBb[i], OAb[i]], [psb[bank]])
                    tg = sbk * 8 + q4 * 4
                    cp("act", Gb.ap[:, tg:tg + 4, :], psb[bank].ap.rearrange("p (t a) -> p t a", t=4), [psb[bank]], [Gb])

        def dense_gen(b, j):
            t0 = j * TT
            hx, hxk, h1T, h1Tk = hxs[j % 2], hxk2[j % 2], h1Ts[j % 2], h1Tk2[j % 2]
            LA = 2
            LD = NUB - 3
            PSH = lambda a: Buf(psb[4 + a % 3].ap[:, 0:TT], psb[4 + a % 3].key)

            def ld(a):
                i = a % NUB
                dma("sp", ubuf[i].ap, u_s[a].rearrange("p (c e) -> p c e", c=8), SCR, [ubuf[i]], "d_ub%d" % i)
                dma("sp", vbuf[i].ap, v_s[a * 128:(a + 1) * 128, :], SCR, [vbuf[i]], "d_vb%d" % i)

            def hid(a):
                i = a % NUB
                if a + LD < 128:
                    ld(a + LD)
                hp = PSH(a)
                for dc in range(8):
                    mm(hp.ap, ubuf[i].ap[:, dc, :], h1T.ap[:, dc, :], dc == 0, dc == 7, [ubuf[i]] + h1Tk, [hp])

            for a in range(LD):
                ld(a)
            for a in range(min(LA, 128)):
                hid(a)
            for a in range(128):
                if a + LA < 128:
                    hid(a + LA)
                i = a % NUB
                k2 = a % 4
                hp = PSH(a)
                act(gel[k2].ap, hp.ap, AF.Gelu_apprx_tanh, [hp], [gel[k2]])
                tt("dve", actb[k2].ap, gel[k2].ap, Gb.ap[:, :, a], ALU.mult, [gel[k2], Gb], [actb[k2]])
                for c in range(2):
                    for hf in range(2):
                        bank = c * 2 + hf
                        mm(psb[bank].ap, actb[k2].ap[:, c * 128:(c + 1) * 128], vbuf[i].ap[:, hf * 512:(hf + 1) * 512],
                           a == 0, a == 127, [actb[k2], vbuf[i]], [psb[bank]])
                yield
            for c in range(2):
                for hf in range(2):
                    bank = c * 2 + hf
                    stt("dve", hx.ap[:, c, hf * 512:(hf + 1) * 512], hx.ap[:, c, hf * 512:(hf + 1) * 512], ALPHA,
                        psb[bank].ap, ALU.mult, ALU.add, [hxk[c], psb[bank]], [hxk[c]])
            run_il(ln_rows(hx, hxk, 2, 0), ln_rows(hx, hxk, 2, 1))
            dma("pool", out_d[b, t0:t0 + TT, :].rearrange("(j p) d -> p j d", p=128), hx.ap, hxk, ["outd"], "d_out")

        def out_h1(b, j):
            t0 = j * TT
            dma("pool", out_d[b, t0:t0 + TT, :].rearrange("(j p) d -> p j d", p=128), hxs[j % 2].ap, hxk2[j % 2], ["outd"], "d_out")

        for b in range(NB):
            pass1(b)
        for b in range(NB):
            S.op("dve", lambda e: e.memset(S_f.ap, 0.0), [], K(S_f))
            if dbg == 1:
                for j in range(NTI):
                    mixer_tile(b, j)
                    out_h1(b, j)
                continue
            mixer_tile(b, 0)
            peer_front(b, 0)
            if dbg == 2:
                continue
            run_il(topk_both())
            if dbg == 3:
                continue
            for j in range(NTI):
                if j + 1 < NTI:
                    mixer_tile(b, j + 1)
                    peer_front(b, j + 1)
                ggen(b, j)
                if dbg == 4:
                    continue
                if j + 2 < NTI:
                    mixer_prefetch(b, j + 2)
                gens = [dense_gen(b, j)]
                if j + 1 < NTI:
                    gens.append(topk_both())
                run_il(*gens)
        S.finish("sp")
        print("ops", S.nops, "waits", S.nwaits, "sbuf_free", nc.sbuf_bytes_remaining)
    return nc


_CACHE = {}


def _consts():
    c = np.zeros((128, 5, 128), np.float32)
    c[:, 0] = np.eye(128)
    c[:, 1] = np.triu(np.ones((128, 128)))
    c[:, 2] = np.tril(np.ones((128, 128)))
    c[:, 3] = np.arange(128)[None, :]
    c[:, 4, 0:16] = np.arange(16)[None, :]
    c[:, 4, 16] = 1.0
    return c


def _shared(inp):
    f = lambda a: np.ascontiguousarray(np.asarray(a, dtype=np.float32))
    rep = lambda a: f(np.broadcast_to(np.asarray(a, np.float32)[None], (128,) + tuple(np.shape(a))))
    u = np.asarray(inp["peer_u"], np.float32)[0].reshape(128, 128, 8, 128)
    return {
        "w_in": f(inp["w_in"][0]), "w_out": f(inp["w_out"][0]), "wq": f(inp["peer_wq"][0]),
        "k1T": f(np.asarray(inp["peer_k1"])[0].transpose(2, 0, 1)),
        "k2T": f(np.asarray(inp["peer_k2"])[0].transpose(2, 0, 1)),
        "uT": f(u.transpose(0, 3, 2, 1).reshape(128, 128, 1024)),
        "vtab": f(inp["peer_v"][0]),
        "lbl": rep(np.asarray(inp["hgrn_lb_logits"])),
        "hgn": rep(np.asarray(inp["hgrn_norm_g"])[0]),
        "glg": rep(np.asarray(inp["gmlp_ln_g"])[0].reshape(512)),
        "glb": rep(np.asarray(inp["gmlp_ln_b"])[0].reshape(512)),
        "wsT": f(np.asarray(inp["gmlp_ws"])[0].transpose(2, 0, 1)),
        "bsT": f(np.asarray(inp["gmlp_bs"])[0].T),
        "lnp": rep(np.stack([np.asarray(inp[k])[0] for k in ("ln1_g", "ln1_b", "ln2_g", "ln2_b")])),
        "cst": _consts(),
    }


def run(inp, ncores, dbg=0):
    x = np.asarray(inp["x"], np.float32)
    B, SEQ, _ = x.shape
    NB = B // ncores
    key = (NB, SEQ, dbg)
    if key not in _CACHE:
        _CACHE[key] = build(NB, SEQ, dbg)
    nc = _CACHE[key]
    sh = _shared(inp)
    maps = []
    for c in range(ncores):
        xs = np.ascontiguousarray(x[c * NB:(c + 1) * NB])
        m = dict(sh)
        m["x"] = xs
        m["xT"] = np.ascontiguousarray(xs.transpose(0, 2, 1))
        maps.append(m)
    res = run_bass_kernel_spmd(nc, maps, core_ids=list(range(ncores)))
    return np.concatenate([r["out"] for r in res.results], axis=0)


def kernel(**inputs):
    return run(inputs, 8).astype(np.float32)
```
